# Optimizing a Trainium2 kernel written in Bass

```python
import jax
import jax.numpy as jnp
from jax import lax
import numpy as np

D_MODEL = 1024
BATCH = 32
SEQ = 2048
DEPTH = 2
DEC_BATCH = 16
DEC_SEQ = 32
PAST_LEN = 4096

CHUNK = 64
Q_BLOCK = 128
D_FF = 2816
NORM_EPS = 1e-6
L2_EPS = 1e-6
GDN_HEADS = 8
GDN_DK = 64
GDN_DV = 64
CONV_W = 4
GDN_QK = GDN_HEADS * GDN_DK
GDN_VW = GDN_HEADS * GDN_DV
GDN_CONV_DIM = 2 * GDN_QK + GDN_VW
MLA_HEADS = 8
Q_RANK = 256
KV_RANK = 128
NOPE_DIM = 64
ROPE_DIM = 32
V_DIM = 64
ROPE_THETA = 10000.0
MLA_OUT = MLA_HEADS * V_DIM
D_MIX = GDN_VW + MLA_OUT
IN_COLS = GDN_CONV_DIM + GDN_VW + 2 * GDN_HEADS + Q_RANK + KV_RANK + ROPE_DIM
MASK_VALUE = -1e30

kernel_name = 'hybrid_gdn_mla_macaron_stream'


def rms_norm(x, w):
    x32 = x.astype(jnp.float32)
    y = x32 * lax.rsqrt(jnp.mean(x32 * x32, axis=-1, keepdims=True) + NORM_EPS)
    return (y * w.astype(jnp.float32)).astype(x.dtype)


def swiglu_ffn(x, w_gate, w_up, w_down):
    return (jax.nn.silu(x @ w_gate) * (x @ w_up)) @ w_down


def rotary(x, pos):
    half = ROPE_DIM // 2
    inv_freq = 1.0 / (ROPE_THETA ** (jnp.arange(half, dtype=jnp.float32) / half))
    ang = pos.astype(jnp.float32)[:, None] * inv_freq[None, :]
    ang = ang.reshape((ang.shape[0],) + (1,) * (x.ndim - 3) + (half,))
    cos, sin = jnp.cos(ang), jnp.sin(ang)
    x32 = x.astype(jnp.float32)
    x1, x2 = x32[..., :half], x32[..., half:]
    return jnp.concatenate([x1 * cos - x2 * sin, x2 * cos + x1 * sin], axis=-1).astype(x.dtype)


def causal_short_conv(x, buf, w):
    xp = jnp.concatenate([buf.astype(x.dtype), x], axis=1)
    y = lax.conv_general_dilated(xp, w[:, None, :].astype(x.dtype), window_strides=(1,), padding='VALID',
                                 dimension_numbers=('NWC', 'WIO', 'NWC'), feature_group_count=x.shape[-1])
    return jax.nn.silu(y), xp[:, -(CONV_W - 1):]


def l2_normalize(x):
    return x * lax.rsqrt(jnp.sum(x * x, axis=-1, keepdims=True) + L2_EPS)


def gated_delta_chunked(q, k, v, g, beta, s0):
    B, T, H, DK = q.shape
    C = CHUNK if T % CHUNK == 0 else T
    N = T // C

    def blk(x):
        return x.reshape(B, N, C, H, -1).transpose(1, 0, 3, 2, 4)

    q = blk(q) * (DK ** -0.5)
    k = blk(k)
    v = blk(v)
    g = jnp.cumsum(g.reshape(B, N, C, H).transpose(1, 0, 3, 2), axis=-1)
    beta = beta.reshape(B, N, C, H).transpose(1, 0, 3, 2)
    causal = jnp.tril(jnp.ones((C, C), dtype=bool))
    strict = jnp.tril(jnp.ones((C, C), dtype=bool), -1)
    diff = g[..., :, None] - g[..., None, :]
    decay = jnp.where(causal, jnp.exp(jnp.where(causal, diff, 0.0)), 0.0)
    kb = k * beta[..., None]
    lower = jnp.where(strict, jnp.einsum('nbhid,nbhjd->nbhij', kb, k) * decay, 0.0)
    a_mat = jnp.eye(C, dtype=jnp.float32) + lower
    rhs = jnp.concatenate([v * beta[..., None], kb * jnp.exp(g)[..., None]], axis=-1)
    sol = lax.linalg.triangular_solve(a_mat, rhs, left_side=True, lower=True)
    value, k_cum = sol[..., :v.shape[-1]], sol[..., v.shape[-1]:]
    attn_intra = jnp.einsum('nbhid,nbhjd->nbhij', q, k) * decay
    q_dec = q * jnp.exp(g)[..., None]
    g_last = g[..., -1]
    k_dec = k * jnp.exp(g_last[..., None] - g)[..., None]

    def step(s, xs):
        qd, ai, val, kc, kd, gl = xs
        v_new = val - jnp.einsum('bhcd,bhde->bhce', kc, s)
        o = jnp.einsum('bhcd,bhde->bhce', qd, s) + jnp.einsum('bhij,bhje->bhie', ai, v_new)
        s = s * jnp.exp(gl)[..., None, None] + jnp.einsum('bhcd,bhce->bhde', kd, v_new)
        return s, o

    s_fin, o = lax.scan(step, s0, (q_dec, attn_intra, value, k_cum, k_dec, g_last))
    o = o.transpose(1, 0, 3, 2, 4).reshape(B, T, H, -1)
    return o, s_fin


def gdn_mixer(qkv, z, a, b, conv_buf, s0, conv_w, a_log, dt_bias, norm_w):
    B, T, _ = qkv.shape
    act, new_buf = causal_short_conv(qkv, conv_buf, conv_w)
    act = act.astype(jnp.float32)
    q = l2_normalize(act[..., :GDN_QK].reshape(B, T, GDN_HEADS, GDN_DK))
    k = l2_normalize(act[..., GDN_QK:2 * GDN_QK].reshape(B, T, GDN_HEADS, GDN_DK))
    v = act[..., 2 * GDN_QK:].reshape(B, T, GDN_HEADS, GDN_DV)
    beta = jax.nn.sigmoid(b.astype(jnp.float32))
    g = -jnp.exp(a_log.astype(jnp.float32)) * jax.nn.softplus(a.astype(jnp.float32) + dt_bias.astype(jnp.float32))
    o, s_new = gated_delta_chunked(q, k, v, g, beta, s0.astype(jnp.float32))
    zf = z.astype(jnp.float32).reshape(B, T, GDN_HEADS, GDN_DV)
    o = o * lax.rsqrt(jnp.mean(o * o, axis=-1, keepdims=True) + NORM_EPS) * norm_w.astype(jnp.float32) * jax.nn.silu(zf)
    return o.reshape(B, T, GDN_VW).astype(qkv.dtype), s_new, new_buf


def chunk_causal_attention(q, k, v, q_pos, k_pos):
    B, Tq, H, Dh = q.shape
    qb = min(Q_BLOCK, Tq)
    nb = Tq // qb
    qs = q.reshape(B, nb, qb, H, Dh).swapaxes(0, 1)
    ps = q_pos.reshape(nb, qb)
    k_chunk = k_pos // CHUNK
    scale = Dh ** -0.5

    def one(args):
        qblk, pblk = args
        s = jnp.einsum('bqhd,bkhd->bhqk', qblk, k, preferred_element_type=jnp.float32) * scale
        allowed = k_chunk[None, :] <= (pblk // CHUNK)[:, None]
        p = jax.nn.softmax(jnp.where(allowed, s, MASK_VALUE), axis=-1)
        return jnp.einsum('bhqk,bkhd->bqhd', p.astype(v.dtype), v)

    o = lax.map(one, (qs, ps))
    return o.swapaxes(0, 1).reshape(B, Tq, H, -1)


def mla_mixer(cq, ckv, kpe, pos, past_ckv, past_kpe, q_norm, kv_norm, w_uq, w_ukv):
    B, T, _ = cq.shape
    q = (rms_norm(cq, q_norm) @ w_uq).reshape(B, T, MLA_HEADS, NOPE_DIM + ROPE_DIM)
    q = jnp.concatenate([q[..., :NOPE_DIM], rotary(q[..., NOPE_DIM:], pos)], axis=-1)
    c_new = rms_norm(ckv, kv_norm)
    kpe_new = rotary(kpe, pos)
    if past_ckv is None:
        c_all, kpe_all, k_pos = c_new, kpe_new, pos
    else:
        c_all = jnp.concatenate([past_ckv.astype(c_new.dtype), c_new], axis=1)
        kpe_all = jnp.concatenate([past_kpe.astype(kpe_new.dtype), kpe_new], axis=1)
        k_pos = jnp.concatenate([jnp.arange(past_ckv.shape[1], dtype=jnp.int32), pos])
    Tk = c_all.shape[1]
    kv = (c_all @ w_ukv).reshape(B, Tk, MLA_HEADS, NOPE_DIM + V_DIM)
    k = jnp.concatenate([kv[..., :NOPE_DIM], jnp.broadcast_to(kpe_all[:, :, None, :], (B, Tk, MLA_HEADS, ROPE_DIM))], axis=-1)
    o = chunk_causal_attention(q, k, kv[..., NOPE_DIM:], pos, k_pos)
    return o.reshape(B, T, MLA_OUT), c_new, kpe_new


def run_trunk(x, pos, past_ckv, past_kpe, s_gdn, s_conv, p):
    c1 = GDN_CONV_DIM
    c2 = c1 + GDN_VW
    c3 = c2 + GDN_HEADS
    c4 = c3 + GDN_HEADS
    c5 = c4 + Q_RANK
    c6 = c5 + KV_RANK
    ckv_rows, kpe_rows, gdn_states, conv_bufs = [], [], [], []
    for l in range(DEPTH):
        x = x + 0.5 * swiglu_ffn(rms_norm(x, p['norm_ffn1'][l]), p['w_ffn1_gate'][l], p['w_ffn1_up'][l], p['w_ffn1_down'][l])
        h = rms_norm(x, p['norm_mix'][l])
        qkv, z, a, b, cq, ckv, kpe = jnp.split(h @ p['w_in'][l], [c1, c2, c3, c4, c5, c6], axis=-1)
        g_out, s_new, buf_new = gdn_mixer(qkv, z, a, b, s_conv[l], s_gdn[l], p['gdn_conv_w'][l],
                                          p['gdn_a_log'][l], p['gdn_dt_bias'][l], p['gdn_norm_w'][l])
        m_out, c_new, kpe_new = mla_mixer(cq, ckv, kpe, pos,
                                          None if past_ckv is None else past_ckv[l],
                                          None if past_kpe is None else past_kpe[l],
                                          p['mla_q_norm'][l], p['mla_kv_norm'][l], p['w_uq'][l], p['w_ukv'][l])
        x = x + jnp.concatenate([g_out, m_out], axis=-1) @ p['w_out'][l]
        x = x + 0.5 * swiglu_ffn(rms_norm(x, p['norm_ffn2'][l]), p['w_ffn2_gate'][l], p['w_ffn2_up'][l], p['w_ffn2_down'][l])
        ckv_rows.append(c_new)
        kpe_rows.append(kpe_new)
        gdn_states.append(s_new.astype(x.dtype))
        conv_bufs.append(buf_new)
    y = rms_norm(x, p['norm_final'])
    return y, jnp.stack(ckv_rows), jnp.stack(kpe_rows), jnp.stack(gdn_states), jnp.stack(conv_bufs)


def setup_inputs(seed: int = 0) -> dict:
    key = jax.random.key(seed)
    ks = jax.random.split(key, 32)
    f32 = jnp.float32

    def normal(k, shape, scale):
        return jax.random.normal(k, shape, f32) * scale

    def gain(k, shape):
        return 1.0 + 0.02 * jax.random.normal(k, shape, f32)

    dt = jnp.exp(jax.random.uniform(ks[14], (DEPTH, GDN_HEADS), f32, float(np.log(1e-3)), float(np.log(1e-1))))
    return {
        'x_prompt': normal(ks[0], (BATCH, SEQ, D_MODEL), 1.0),
        'x_sample': normal(ks[1], (DEC_BATCH, DEC_SEQ, D_MODEL), 1.0),
        'cache_mla_ckv': normal(ks[2], (DEPTH, DEC_BATCH, PAST_LEN, KV_RANK), 1.0),
        'cache_mla_krope': normal(ks[3], (DEPTH, DEC_BATCH, PAST_LEN, ROPE_DIM), 1.0),
        'state_gdn': normal(ks[4], (DEPTH, DEC_BATCH, GDN_HEADS, GDN_DK, GDN_DV), GDN_DK ** -0.5),
        'state_gdn_conv': normal(ks[5], (DEPTH, DEC_BATCH, CONV_W - 1, GDN_CONV_DIM), 1.0),
        'norm_ffn1': gain(ks[6], (DEPTH, D_MODEL)),
        'w_ffn1_gate': normal(ks[7], (DEPTH, D_MODEL, D_FF), D_MODEL ** -0.5),
        'w_ffn1_up': normal(ks[8], (DEPTH, D_MODEL, D_FF), D_MODEL ** -0.5),
        'w_ffn1_down': normal(ks[9], (DEPTH, D_FF, D_MODEL), D_FF ** -0.5),
        'norm_mix': gain(ks[10], (DEPTH, D_MODEL)),
        'w_in': normal(ks[11], (DEPTH, D_MODEL, IN_COLS), D_MODEL ** -0.5),
        'gdn_conv_w': normal(ks[12], (DEPTH, CONV_W, GDN_CONV_DIM), CONV_W ** -0.5),
        'gdn_a_log': jnp.log(jax.random.uniform(ks[13], (DEPTH, GDN_HEADS), f32, 1.0, 16.0)),
        'gdn_dt_bias': dt + jnp.log(-jnp.expm1(-dt)),
        'gdn_norm_w': gain(ks[15], (DEPTH, GDN_DV)),
        'mla_q_norm': gain(ks[16], (DEPTH, Q_RANK)),
        'mla_kv_norm': gain(ks[17], (DEPTH, KV_RANK)),
        'w_uq': normal(ks[18], (DEPTH, Q_RANK, MLA_HEADS * (NOPE_DIM + ROPE_DIM)), Q_RANK ** -0.5),
        'w_ukv': normal(ks[19], (DEPTH, KV_RANK, MLA_HEADS * (NOPE_DIM + V_DIM)), KV_RANK ** -0.5),
        'w_out': normal(ks[20], (DEPTH, D_MIX, D_MODEL), D_MIX ** -0.5),
        'norm_ffn2': gain(ks[21], (DEPTH, D_MODEL)),
        'w_ffn2_gate': normal(ks[22], (DEPTH, D_MODEL, D_FF), D_MODEL ** -0.5),
        'w_ffn2_up': normal(ks[23], (DEPTH, D_MODEL, D_FF), D_MODEL ** -0.5),
        'w_ffn2_down': normal(ks[24], (DEPTH, D_FF, D_MODEL), D_FF ** -0.5),
        'norm_final': gain(ks[25], (D_MODEL,)),
    }


def reference(x_prompt, x_sample, cache_mla_ckv, cache_mla_krope, state_gdn, state_gdn_conv,
              norm_ffn1, w_ffn1_gate, w_ffn1_up, w_ffn1_down, norm_mix, w_in, gdn_conv_w, gdn_a_log,
              gdn_dt_bias, gdn_norm_w, mla_q_norm, mla_kv_norm, w_uq, w_ukv, w_out, norm_ffn2,
              w_ffn2_gate, w_ffn2_up, w_ffn2_down, norm_final):
    params = dict(norm_ffn1=norm_ffn1, w_ffn1_gate=w_ffn1_gate, w_ffn1_up=w_ffn1_up, w_ffn1_down=w_ffn1_down,
                  norm_mix=norm_mix, w_in=w_in, gdn_conv_w=gdn_conv_w, gdn_a_log=gdn_a_log,
                  gdn_dt_bias=gdn_dt_bias, gdn_norm_w=gdn_norm_w, mla_q_norm=mla_q_norm,
                  mla_kv_norm=mla_kv_norm, w_uq=w_uq, w_ukv=w_ukv, w_out=w_out, norm_ffn2=norm_ffn2,
                  w_ffn2_gate=w_ffn2_gate, w_ffn2_up=w_ffn2_up, w_ffn2_down=w_ffn2_down, norm_final=norm_final)
    b_p, t_p = x_prompt.shape[0], x_prompt.shape[1]
    zero_state = jnp.zeros((DEPTH, b_p, GDN_HEADS, GDN_DK, GDN_DV), jnp.float32)
    zero_conv = jnp.zeros((DEPTH, b_p, CONV_W - 1, GDN_CONV_DIM), x_prompt.dtype)
    y_prompt, p_ckv, p_kpe, p_gdn, p_conv = run_trunk(x_prompt, jnp.arange(t_p, dtype=jnp.int32), None, None,
                                                      zero_state, zero_conv, params)
    pos_s = cache_mla_ckv.shape[2] + jnp.arange(x_sample.shape[1], dtype=jnp.int32)
    y_sample, s_ckv, s_kpe, s_gdn, s_conv = run_trunk(x_sample, pos_s, cache_mla_ckv, cache_mla_krope,
                                                      state_gdn, state_gdn_conv, params)
    return (y_prompt, y_sample, p_ckv, p_kpe, p_gdn, p_conv, s_ckv, s_kpe, s_gdn, s_conv)
```

```python
import os
import numpy as np
import ml_dtypes
import concourse.bass as bass
import concourse.mybir as mybir
from concourse.bass_utils import run_bass_kernel_spmd

F32 = mybir.dt.float32
BF16 = mybir.dt.bfloat16
AF = mybir.ActivationFunctionType
ALU = mybir.AluOpType
AX = mybir.AxisListType

NCORES = 8
D = 1024
DFF = 2816
NM = DFF // 128
H = 8
DK = 64
QR = 256
KVR = 128
ROPE = 32
NOPE = 64
INC = 2480
CONVD = 1536
EPS = 1e-6
SLOT = 2048
RING = 3
NDMA = 6
STOP = int(os.environ.get('DBG_STOP', '9'))
GSTOP = float(os.environ.get('DBG_GSTOP', '9'))


class Sched:
    ENG = ("pe", "act", "dve", "pool", "sp")

    def __init__(self, nc, dry):
        self.nc = nc
        self.dry = dry
        self.cnt = {e: 0 for e in self.ENG}
        self.seen = {e: {} for e in self.ENG}
        self.prog = {e: [] for e in self.ENG}
        self.tok_w = {}
        self.tok_r = {}
        self.dma_val = {}
        self.dma_rr = {"sp": 0, "pool": 0}
        self.sems = {}
        if not dry:
            for e in self.ENG:
                self.sems[e] = nc.alloc_semaphore(name=f"sq_{e}")
            for q in ("sp", "pool"):
                for k in range(NDMA):
                    self.sems[("d", q, k)] = nc.alloc_semaphore(name=f"sd_{q}_{k}")

    def _deps(self, e, R, W):
        waits = {}

        def need(dep):
            if dep is None:
                return
            sk, val = dep
            if sk == e and e == "pe":
                return
            if self.seen[e].get(sk, 0) >= val:
                return
            if waits.get(sk, 0) < val:
                waits[sk] = val

        for t in R:
            need(self.tok_w.get(t))
        for t in W:
            need(self.tok_w.get(t))
            for d in self.tok_r.get(t, ()):
                need(d)
        return waits

    def _commit(self, me, R, W):
        for t in R:
            self.tok_r.setdefault(t, []).append(me)
        for t in W:
            self.tok_w[t] = me
            self.tok_r[t] = []

    def op(self, e, fn, R=(), W=()):
        if self.dry:
            return
        waits = self._deps(e, R, W)
        for sk, v in waits.items():
            self.seen[e][sk] = v
        self.cnt[e] += 1
        me = (e, self.cnt[e])
        self.seen[e][e] = max(self.seen[e].get(e, 0), 0)
        self.prog[e].append((list(waits.items()), fn, (e, 1)))
        self._commit(me, R, W)

    def dma(self, q, out, in_, R=(), W=(), slow=False):
        if self.dry:
            return
        k = self.dma_rr[q]
        self.dma_rr[q] = (k + 1) % NDMA
        sk = ("d", q, k)
        prev = self.dma_val.get(sk, 0)
        waits = self._deps(q, R, W)
        if prev > 0 and self.seen[q].get(sk, 0) < prev:
            waits[sk] = max(waits.get(sk, 0), prev)
        for s2, v in waits.items():
            self.seen[q][s2] = v
        self.dma_val[sk] = prev + 16
        me = (sk, prev + 16)
        fn = (lambda eng, o=out, i=in_: eng.dma_start(out=o, in_=i, allow_slow_non_contiguous=True)) if slow else (lambda eng, o=out, i=in_: eng.dma_start(out=o, in_=i))
        self.prog[q].append((list(waits.items()), fn, (sk, 16)))
        self._commit(me, R, W)

    def barrier(self, engines=("pe", "act", "dve", "pool")):
        if self.dry:
            return
        for e in engines:
            waits = {}
            for o in engines:
                if o != e and self.cnt[o] > self.seen[e].get(o, 0):
                    waits[o] = self.cnt[o]
            for sk, v in self.dma_val.items():
                if self.seen[e].get(sk, 0) < v:
                    waits[sk] = v
            for s2, v in waits.items():
                self.seen[e][s2] = v
            if waits:
                self.prog[e].append((list(waits.items()), None, None))
        waits = {}
        for o in engines:
            if self.cnt[o] > self.seen["sp"].get(o, 0):
                waits[o] = self.cnt[o]
        for s2, v in waits.items():
            self.seen["sp"][s2] = v
        if waits:
            self.prog["sp"].append((list(waits.items()), None, None))

    def finish(self):
        waits = {}
        for sk, v in self.dma_val.items():
            waits[sk] = v
        for e in ("pe", "act", "dve", "pool"):
            waits[e] = self.cnt[e]
        self.prog["sp"].append((list(waits.items()), None, None))
        self.prog["pool"].append((list(waits.items()), None, None))

    def replay(self, e, eng):
        for waits, fn, inc in self.prog[e]:
            for sk, v in waits:
                if v > 0:
                    eng.wait_ge(self.sems[sk], v)
            if fn is not None:
                ins = fn(eng)
                ins.then_inc(self.sems[inc[0]], inc[1])


class Arena:
    def __init__(self, ap_f32):
        self.ap = ap_f32
        self.n = ap_f32.shape[1]
        self.pos = 0

    def alloc(self, cols, dtype=F32):
        w = cols if dtype == F32 else (cols + 1) // 2
        w = (w + 7) // 8 * 8
        assert self.pos + w <= self.n, ("arena overflow", self.pos, w, self.n)
        v = self.ap[:, self.pos:self.pos + w]
        self.pos += w
        if dtype != F32:
            v = v.bitcast(dtype)[:, :cols]
        else:
            v = v[:, :cols]
        return v

    def mark(self):
        return self.pos

    def reset(self, m):
        self.pos = m


def _slot_plan(depth):
    keys = []
    for l in range(depth):
        for f in (1, 2):
            for m in range(NM):
                keys.append(("A", l, f, m))
            for half in range(2):
                for j in range(6):
                    keys.append(("B", l, f, half, j))
        for j in range(6):
            keys.append(("F", l, j))
        for j in range(4):
            keys.append(("T", l, j))
        keys.append(("UQ", l))
        keys.append(("UKV", l))
        for j in range(4):
            keys.append(("O", l, j))
    return keys


class Builder:
    def __init__(self, nc, shp, dry, plan=None):
        self.nc = nc
        self.shp = shp
        self.dry = dry
        self.S = Sched(nc, dry)
        self.plan = plan
        self.req = []
        self.pos = 0
        self.issued = 0
        self.bank_rr = 0
        self.held = set()

    def mm(self, out, lhsT, rhs, start, stop, R, W):
        self.S.op("pe", lambda t, o=out, a=lhsT, b=rhs, s=start, p=stop: t.matmul(o, lhsT=a, rhs=b, start=s, stop=p), R, W)

    def tr(self, out, in_, ident, R, W):
        self.S.op("pe", lambda t, o=out, a=in_, b=ident: t.transpose(o, a, b), R, W)

    def act(self, out, in_, func, R, W, bias=None, scale=None):
        kw = {}
        if bias is not None:
            kw["bias"] = bias
        if scale is not None:
            kw["scale"] = scale
        self.S.op("act", lambda a, o=out, i=in_, f=func, k=kw: a.activation(out=o, in_=i, func=f, **k), R, W)

    def tt(self, e, out, in0, in1, op, R, W):
        self.S.op(e, lambda v, o=out, a=in0, b=in1, p=op: v.tensor_tensor(out=o, in0=a, in1=b, op=p), R, W)

    def ts(self, e, out, in0, s1, s2, op0, op1, R, W):
        if s2 is None:
            self.S.op(e, lambda v, o=out, a=in0, x=s1, p0=op0: v.tensor_scalar(out=o, in0=a, scalar1=x, scalar2=None, op0=p0), R, W)
        else:
            self.S.op(e, lambda v, o=out, a=in0, x=s1, y=s2, p0=op0, p1=op1: v.tensor_scalar(out=o, in0=a, scalar1=x, scalar2=y, op0=p0, op1=p1), R, W)

    def stt(self, e, out, in0, scalar, in1, op0, op1, R, W):
        self.S.op(e, lambda v, o=out, a=in0, s=scalar, b=in1, p0=op0, p1=op1: v.scalar_tensor_tensor(out=o, in0=a, scalar=s, in1=b, op0=p0, op1=p1), R, W)

    def cp(self, e, out, in_, R, W):
        if e == "act":
            self.S.op("act", lambda a, o=out, i=in_: a.copy(out=o, in_=i), R, W)
        else:
            self.S.op(e, lambda v, o=out, i=in_: v.tensor_copy(out=o, in_=i), R, W)

    def red(self, out, in_, R, W):
        self.S.op("dve", lambda v, o=out, i=in_: v.tensor_reduce(out=o, in_=i, axis=AX.X, op=ALU.add), R, W)

    def rstd(self, ap, scale, tok):
        self.act(ap, ap, AF.Ln, [tok], [tok], bias=EPS, scale=scale)
        self.act(ap, ap, AF.Exp, [tok], [tok], scale=-0.5)

    def memset(self, e, ap, val, W):
        self.S.op(e, lambda v, a=ap, c=val: v.memset(a, c), (), W)

    def bank(self, hold=False):
        i = self.bank_rr
        while i in self.held:
            i = (i + 1) % 8
        self.bank_rr = (i + 1) % 8
        if hold:
            self.held.add(i)
        return self.ps[i], ("ps", i)

    def release(self, *toks):
        for t in toks:
            self.held.discard(t[1])

    def fetch(self, key):
        if self.dry:
            self.req.append(key)
            return self.wring[:, 0, :], ("wr", 0)
        i = self.pos
        assert self.plan[i] == key, (i, self.plan[i], key)
        last = min(i + RING - 1, len(self.plan) - 1)
        while self.issued <= last:
            j = self.issued
            sid = self.slot_id[self.plan[j]]
            self.S.dma("sp", self.wring[:, j % RING, :], self.wscr[sid], R=[("wscr", sid)], W=[("wr", j % RING)])
            self.issued += 1
        self.pos += 1
        return self.wring[:, i % RING, :], ("wr", i % RING)

    def build(self):
        nc, shp = self.nc, self.shp
        depth, NP, T, NS, TS, PAST = shp["depth"], shp["NP"], shp["T"], shp["NS"], shp["TS"], shp["PAST"]
        self.depth = depth
        dt = lambda name, shape, dtype=F32, kind="ExternalInput": nc.dram_tensor(name, list(shape), dtype, kind=kind).ap()
        I = {}
        I["xp"] = dt("xp", [NP, T, D]); I["xs"] = dt("xs", [NS, TS, D])
        I["cckv"] = dt("cckv", [depth, NS, PAST, KVR]); I["ckr"] = dt("ckr", [depth, NS, PAST, ROPE])
        I["sg"] = dt("sg", [depth, NS, H, DK, DK]); I["sc"] = dt("sc", [depth, NS, 3, CONVD])
        for f in (1, 2):
            I[f"nf{f}"] = dt(f"nf{f}", [depth, D]); I[f"wg{f}"] = dt(f"wg{f}", [depth, D, DFF])
            I[f"wu{f}"] = dt(f"wu{f}", [depth, D, DFF]); I[f"wd{f}"] = dt(f"wd{f}", [depth, DFF, D])
        I["nm"] = dt("nm", [depth, D]); I["win"] = dt("win", [depth, D, INC]); I["cw"] = dt("cw", [depth, 4, CONVD])
        I["alog"] = dt("alog", [depth, H]); I["dtb"] = dt("dtb", [depth, H]); I["gnw"] = dt("gnw", [depth, DK])
        I["qn"] = dt("qn", [depth, QR]); I["kvn"] = dt("kvn", [depth, KVR]); I["wuq"] = dt("wuq", [depth, QR, H * 96])
        I["wukv"] = dt("wukv", [depth, KVR, H * 128]); I["wo"] = dt("wo", [depth, D, D]); I["nfin"] = dt("nfin", [1, D])
        I["rotp"] = dt("rotp", [T, 32]); I["rots"] = dt("rots", [TS, 32])
        I["cmask"] = dt("cmask", [2, 6, 128, 128]); I["identb"] = dt("identb", [128, 128], BF16)
        O = {}
        ko = "ExternalOutput"
        O["yp"] = dt("yp", [NP, T, D], kind=ko); O["ys"] = dt("ys", [NS, TS, D], kind=ko)
        O["pckv"] = dt("pckv", [depth, NP, T, KVR], kind=ko); O["pkpe"] = dt("pkpe", [depth, NP, T, ROPE], kind=ko)
        O["pgdn"] = dt("pgdn", [depth, NP, H, DK, DK], kind=ko); O["pconv"] = dt("pconv", [depth, NP, 3, CONVD], kind=ko)
        O["sckv"] = dt("sckv", [depth, NS, TS, KVR], kind=ko); O["skpe"] = dt("skpe", [depth, NS, TS, ROPE], kind=ko)
        O["sgdn"] = dt("sgdn", [depth, NS, H, DK, DK], kind=ko); O["sconv"] = dt("sconv", [depth, NS, 3, CONVD], kind=ko)
        self.I, self.O = I, O
        keys = _slot_plan(depth)
        self.slot_id = {k: i for i, k in enumerate(keys)}
        self.wscr = nc.dram_tensor("wscr", [len(keys), 128, SLOT], BF16, kind="Internal").ap()

        NBLK = max(T // 128, 1)
        self.ps = [nc.alloc_psum_tensor(f"psb{i}", [128, 512], F32)[:] for i in range(8)]
        xall = nc.alloc_sbuf_tensor("xres", [128, NBLK * D], F32)[:]
        self.X = xall.rearrange("p (b d) -> p b d", d=D)
        self.wring = nc.alloc_sbuf_tensor("wring", [128, RING * SLOT], BF16)[:].rearrange("p (r s) -> p r s", s=SLOT)
        avail = (nc.sbuf_top - nc.sbuf_base) - 2048
        ar = nc.alloc_sbuf_tensor("arena", [128, avail // 4], F32)[:]
        self.A = Arena(ar)
        A = self.A
        self.cm = A.alloc(2 * 6 * 128).rearrange("p (s k c) -> p s k c", s=2, k=6)
        self.identb = A.alloc(128, BF16)
        self.identf = self.cm[:, 0, 5, :]
        self.gT = A.alloc(depth * 3 * 8)
        self.cwT = A.alloc(depth * 12 * 4)
        self.smallbc = A.alloc(depth * (8 + 8 + 64 + QR + KVR))
        self.negA = A.alloc(depth * 8)
        S = self.S
        if True:
            S.dma("pool", self.cm, I["cmask"].rearrange("s k p c -> p s k c"), W=["const"])
            S.dma("pool", self.identb, I["identb"], W=["const"])
            for l in range(depth):
                for wi, nm in enumerate(("nf1", "nm", "nf2")):
                    o = (l * 3 + wi) * 8
                    S.dma("pool", self.gT[:, o:o + 8], I[nm][l].rearrange("(k p) -> p k", p=128), W=["const"], slow=True)
                for j in range(4):
                    S.dma("pool", self.cwT[:, l * 48:(l + 1) * 48].rearrange("p (c j) -> p c j", j=4)[:, :, j],
                          I["cw"][l][j].rearrange("(c p) -> p c", p=128), W=["const"], slow=True)
                o = l * (16 + 64 + QR + KVR)
                sb = self.smallbc
                S.dma("pool", sb[:, o:o + 8], I["alog"][l:l + 1, :].partition_broadcast(128), W=["const"])
                S.dma("pool", sb[:, o + 8:o + 16], I["dtb"][l:l + 1, :].partition_broadcast(128), W=["const"])
                S.dma("pool", sb[:, o + 16:o + 80], I["gnw"][l:l + 1, :].partition_broadcast(128), W=["const"])
                S.dma("pool", sb[:, o + 80:o + 80 + QR], I["qn"][l:l + 1, :].partition_broadcast(128), W=["const"])
                S.dma("pool", sb[:, o + 80 + QR:o + 80 + QR + KVR], I["kvn"][l:l + 1, :].partition_broadcast(128), W=["const"])
        for l in range(depth):
            o = l * (16 + 64 + QR + KVR)
            self.act(self.negA[:, l * 8:(l + 1) * 8], self.smallbc[:, o:o + 8], AF.Exp, ["const"], ["negA"])
        self.ts("dve", self.negA, self.negA, -1.0, None, ALU.mult, None, ["negA"], ["negA"])

        self.prepass()
        base = A.mark()
        for b in range(NP):
            A.reset(base)
            self.sequence(b, T, False)
        for b in range(NS):
            A.reset(base)
            self.sequence(b, TS, True)
        S.finish()

    def bc(self, l, which):
        o = l * (16 + 64 + QR + KVR)
        off, n = {"alog": (0, 8), "dtb": (8, 8), "gnw": (16, 64), "qn": (80, QR), "kvn": (80 + QR, KVR)}[which]
        return self.smallbc[:, o + off:o + off + n]

    def prepass(self):
        S, I, A = self.S, self.I, self.A
        m0 = A.mark()
        stg = [A.alloc(SLOT, BF16) for _ in range(3)]
        n = 0
        for key, sid in self.slot_id.items():
            st = stg[n % 3]
            tok = ("stg", n % 3)
            n += 1
            parts = []
            kind, l = key[0], key[1]
            if kind == "A":
                f, m = key[2], key[3]
                for r, wn in enumerate(("wg", "wu")):
                    parts.append((st[:, r * 1024:(r + 1) * 1024].rearrange("p (k c) -> p k c", c=128),
                                  I[f"{wn}{f}"][l][:, m * 128:(m + 1) * 128].rearrange("(k p) c -> p k c", p=128)))
            elif kind == "B":
                f, half, j = key[2], key[3], key[4]
                nr = min(4, NM - 4 * j)
                parts.append((st[:, 0:nr * 512].rearrange("p (r n) -> p r n", n=512),
                              I[f"wd{f}"][l][4 * j * 128:(4 * j + nr) * 128, half * 512:(half + 1) * 512].rearrange("(r p) n -> p r n", p=128)))
            elif kind == "F":
                j = key[2]
                for r in range(2):
                    c = 2 * j + r
                    parts.append((st[:, r * 1024:(r + 1) * 1024].rearrange("p (k c) -> p k c", c=128),
                                  I["win"][l][:, c * 128:(c + 1) * 128].rearrange("(k p) c -> p k c", p=128)))
            elif kind == "T":
                j = key[2]
                parts.append((st[:, 0:2 * 944].rearrange("p (r n) -> p r n", n=944),
                              I["win"][l][2 * j * 128:(2 * j + 2) * 128, CONVD:INC].rearrange("(r p) n -> p r n", p=128)))
            elif kind == "UQ":
                parts.append((st[:, 0:2 * 768].rearrange("p (r n) -> p r n", n=768),
                              I["wuq"][l].rearrange("(r p) n -> p r n", p=128)))
            elif kind == "UKV":
                parts.append((st[:, 0:1024], I["wukv"][l]))
            elif kind == "O":
                j = key[2]
                parts.append((st[:, 0:2048].rearrange("p (r n) -> p r n", n=1024),
                              I["wo"][l][2 * j * 128:(2 * j + 2) * 128, :].rearrange("(r p) n -> p r n", p=128)))
            for o_, i_ in parts:
                S.dma("pool", o_, i_, W=[tok])
            S.dma("sp", self.wscr[sid], st, R=[tok], W=[("wscr", sid)])
        A.reset(m0)

    def norm_stats(self, blk0, nblk, PB, sq, ss, xrs):
        for b in range(nblk):
            xb = self.X[:PB, blk0 + b, :]
            xt = ("X", blk0 + b)
            self.tt("dve", sq[:PB, :], xb, xb, ALU.mult, [xt], ["nsq"])
            self.red(ss[:PB, b:b + 1], sq[:PB, :], ["nsq"], [("nss", b)])
            self.rstd(ss[:PB, b:b + 1], 1.0 / D, ("nss", b))
            self.ts("dve", xrs[b][:PB, :], xb, ss[:PB, b:b + 1], None, ALU.mult, None, [xt, ("nss", b)], [("nxr", b)])

    def norm_tr(self, xnT, xtok, gi, nblk, PB, xrs):
        for b in range(nblk):
            pb, pt = self.bank()
            pbv = pb.bitcast(BF16)
            for k in range(8):
                self.tr(pbv[:, k * PB:(k + 1) * PB], xrs[b][:PB, k * 128:(k + 1) * 128], self.identb[:PB, :PB], [("nxr", b), "const"], [pt])
            self.tt("dve", xnT[:, :, b * PB:(b + 1) * PB], pbv[:, 0:8 * PB].rearrange("p (k t) -> p k t", t=PB),
                    self.gT[:, gi:gi + 8].unsqueeze(2).to_broadcast([128, 8, PB]), ALU.mult, [pt, "const"], [xtok])

    def ffn_A(self, l, f, nblk, PB, xnT, xtok, hT):
        TT = nblk * PB
        for m in range(NM):
            w, wt = self.fetch(("A", l, f, m))
            pg, pgt = self.bank()
            pu, put = self.bank()
            for k in range(8):
                self.mm(pg[:, :TT], w[:, k * 128:(k + 1) * 128], xnT[:, k, :TT], k == 0, k == 7, [wt, xtok], [pgt])
            for k in range(8):
                self.mm(pu[:, :TT], w[:, 1024 + k * 128:1024 + (k + 1) * 128], xnT[:, k, :TT], k == 0, k == 7, [wt, xtok], [put])
            sg = self.sgb[m % 2]
            self.act(sg[:, :TT], pg[:, :TT], AF.Silu, [pgt], [("sg", m % 2)])
            self.tt("dve", hT[:, m, :TT], sg[:, :TT], pu[:, :TT], ALU.mult, [("sg", m % 2), put], [("hT", m)])

    def ffn_B(self, l, f, blk0, nblk, PB, hT):
        for half in range(2):
            banks = [self.bank(hold=True) for _ in range(nblk)]
            for j in range(6):
                w, wt = self.fetch(("B", l, f, half, j))
                for r in range(min(4, NM - 4 * j)):
                    m = 4 * j + r
                    for b in range(nblk):
                        self.mm(banks[b][0][:PB, :], hT[:, m, b * PB:(b + 1) * PB], w[:, r * 512:(r + 1) * 512],
                                m == 0, m == NM - 1, [wt, ("hT", m)], [banks[b][1]])
            for b in range(nblk):
                xs = self.X[:PB, blk0 + b, half * 512:(half + 1) * 512]
                self.stt("dve", xs, banks[b][0][:PB, :], 0.5, xs, ALU.mult, ALU.add, [banks[b][1]], [("X", blk0 + b)])
                self.release(banks[b][1])


    def sequence(self, b, T, sample):
        S, I, O, A = self.S, self.I, self.O, self.A
        PB = min(128, T)
        NBLK = T // PB
        BPT = min(4, NBLK)
        NT = NBLK // BPT
        TT = BPT * PB
        C = 64 if T % 64 == 0 else T
        xin = (I["xs"] if sample else I["xp"])[b]
        for blk in range(NBLK):
            S.dma("pool", self.X[:PB, blk, :], xin[blk * PB:(blk + 1) * PB, :], W=[("X", blk)])
        for l in range(self.depth):
            self.ffn_phase(l, 1, NT, BPT, PB)
            self.mixer_phase(l, b, T, sample, NT, BPT, PB, C)
            self.ffn_phase(l, 2, NT, BPT, PB)
        S.barrier()
        m0 = A.mark()
        sq = A.alloc(D); ss = A.alloc(8); yo = [A.alloc(D) for _ in range(2)]
        gfin = A.alloc(D)
        S.dma("pool", gfin, I["nfin"][0:1, :].partition_broadcast(128), W=["gfin"])
        yout = (O["ys"] if sample else O["yp"])[b]
        for blk in range(NBLK):
            xb = self.X[:PB, blk, :]
            xt = ("X", blk)
            self.tt("dve", sq[:PB, :], xb, xb, ALU.mult, [xt], ["nsq"])
            self.red(ss[:PB, 0:1], sq[:PB, :], ["nsq"], ["nss"])
            self.rstd(ss[:PB, 0:1], 1.0 / D, "nss")
            y = yo[blk % 2]
            self.stt("dve", y[:PB, :], xb, ss[:PB, 0:1], gfin[:PB, :], ALU.mult, ALU.mult, [xt, "nss", "gfin"], [("yo", blk % 2)])
            S.dma("pool", yout[blk * PB:(blk + 1) * PB, :], y[:PB, :], R=[("yo", blk % 2)], W=[])
        S.barrier()
        A.reset(m0)

    def ffn_phase(self, l, f, NT, BPT, PB):
        S, A = self.S, self.A
        S.barrier()
        m0 = A.mark()
        xnTs = [A.alloc(8 * 512, BF16).rearrange("p (k t) -> p k t", k=8) for _ in range(2)]
        hT = A.alloc(NM * 512, BF16).rearrange("p (m t) -> p m t", m=NM)
        self.sgb = [A.alloc(512) for _ in range(2)]
        sq, ss = A.alloc(D), A.alloc(8)
        xrs = [A.alloc(D, BF16) for _ in range(BPT)]
        gi = (l * 3 + (0 if f == 1 else 2)) * 8
        self.norm_stats(0, BPT, PB, sq, ss, xrs)
        self.norm_tr(xnTs[0], ("xnT", 0), gi, BPT, PB, xrs)
        for tt in range(NT):
            cur = tt % 2
            self.ffn_A(l, f, BPT, PB, xnTs[cur], ("xnT", cur), hT)
            if tt + 1 < NT:
                self.norm_stats((tt + 1) * BPT, BPT, PB, sq, ss, xrs)
            self.ffn_B(l, f, tt * BPT, BPT, PB, hT)
            if tt + 1 < NT:
                self.norm_tr(xnTs[1 - cur], ("xnT", 1 - cur), gi, BPT, PB, xrs)
        S.barrier()
        A.reset(m0)


    def mixer_phase(self, l, b, T, sample, NT, BPT, PB, C):
        mixer_phase_impl(self, l, b, T, sample, NT, BPT, PB, C)


def mixer_phase_impl(self, l, b, T, sample, NT, BPT, PB, C):
    S, A, I, O = self.S, self.A, self.I, self.O
    S.barrier()
    m0 = A.mark()
    NBLK = T // PB
    BM = 1 if T > 1024 else min(2, NBLK)
    TM = BM * PB
    NTM = NBLK // BM
    NCH = TM // C
    cs = 1 if sample else 0
    triU, strictL, onesblk, ones = (self.cm[:, cs, k, :] for k in (0, 1, 3, 4))
    identf = self.cm[:, cs, 5, :]
    identb = self.identb
    SC = 1.0 / float(np.sqrt(96.0))
    PAST = self.shp["PAST"] if sample else 0
    sfx = "s" if sample else "p"
    o_ckv, o_kpe, o_gdn, o_conv = (O[sfx + n][l, b] for n in ("ckv", "kpe", "gdn", "conv"))
    xnT = A.alloc(8 * TM, BF16).rearrange("p (k t) -> p k t", k=8)
    mixT = A.alloc(8 * TM, BF16).rearrange("p (k t) -> p k t", k=8)
    scr = (A.alloc(D), A.alloc(8))
    xrs = [A.alloc(D, BF16) for _ in range(BM)]
    qkvT = A.alloc(12 * TM).rearrange("p (c t) -> p c t", c=12)
    rw = [A.alloc(TM + 3) for _ in range(2)]
    cacc = A.alloc(TM); csab = [A.alloc(TM) for _ in range(2)]
    csq_all = scr[0] if 8 * TM <= D else A.alloc(8 * TM)
    rk = A.alloc(64)
    halo = A.alloc(36).rearrange("p (c j) -> p c j", j=3)
    wT = A.alloc(4 * SLOT, BF16).rearrange("p (j s) -> p j s", s=SLOT)
    wuq = A.alloc(2 * 768, BF16)
    wukv = A.alloc(1024, BF16)
    rot = A.alloc(NBLK * 32).rearrange("p (b c) -> p b c", c=32)
    KTt = A.alloc(8 * 512, BF16).rearrange("p (h t) -> p h t", h=8)
    Vt = A.alloc(4 * 8 * 65, BF16).rearrange("p (k h d) -> p k h d", k=4, h=8)
    QT = A.alloc(8 * TM, BF16).rearrange("p (h t) -> p h t", h=8)
    if sample:
        cst = A.alloc(4 * 128).rearrange("p (k d) -> p k d", k=4)
        kst = A.alloc(4 * 32).rearrange("p (k d) -> p k d", k=4)
        cbs = A.alloc(4 * 128, BF16).rearrange("p (k d) -> p k d", k=4)
        kbs = A.alloc(4 * 96, BF16).rearrange("p (k d) -> p k d", k=4)
        cT = A.alloc(512, BF16); kpT = A.alloc(512, BF16)
        cTn = A.alloc(T, BF16); kpTn = A.alloc(T, BF16)
    else:
        cT = A.alloc(T, BF16); kpT = A.alloc(T, BF16)
        cTn, kpTn = cT, kpT
    cqf = A.alloc(QR); cqn = A.alloc(QR, BF16); cqnT = A.alloc(2 * 128, BF16).rearrange("p (k t) -> p k t", k=2)
    qf = A.alloc(768); qb = A.alloc(768, BF16)
    rt = [A.alloc(8 * 16).rearrange("p (h d) -> p h d", h=8) for _ in range(4)]
    cf = A.alloc(KVR); cnew = [A.alloc(KVR) for _ in range(2)]; cb = A.alloc(KVR, BF16)
    kf = A.alloc(32); kn = [A.alloc(32) for _ in range(2)]; kb96 = A.alloc(96, BF16)
    sm = A.alloc(64)
    eT = [A.alloc(8 * 128, BF16).rearrange("p (h q) -> p h q", h=8) for _ in range(2)]
    oacc = [A.alloc(8 * 65).rearrange("p (h d) -> p h d", h=8) for _ in range(BM)]
    mo = A.alloc(512, BF16)
    mog = A.alloc(512, BF16)
    gball = A.alloc(14 * 512)
    gb = [gball[:, i * 512:(i + 1) * 512] for i in range(14)]
    Sst = A.alloc(512).rearrange("p (h d) -> p h d", h=8)
    gsm = A.alloc(64)
    qodd = A.alloc(8 * 64)
    cbuf = gball[:, 11 * 512:14 * 512]
    G = lambda i: gb[i]
    GT = lambda i: ("gb", i)
    g3 = lambda i, n: gb[i][:, 0:8 * n].rearrange("p (h d) -> p h d", h=8)

    for j in range(4):
        S.dma("sp", wT[:, j, :], self.wscr[self.slot_id[("T", l, j)]], R=[("wscr", self.slot_id[("T", l, j)])], W=["wT"])
    S.dma("sp", wuq, self.wscr[self.slot_id[("UQ", l)]][:, 0:1536], R=[("wscr", self.slot_id[("UQ", l)])], W=["wuq"])
    S.dma("sp", wukv, self.wscr[self.slot_id[("UKV", l)]][:, 0:1024], R=[("wscr", self.slot_id[("UKV", l)])], W=["wukv"])
    rsrc = I["rots"] if sample else I["rotp"]
    S.dma("pool", rot[:PB, :, :], rsrc.rearrange("(b p) c -> p b c", p=PB), W=["rot"])
    self.memset("pool", Vt[:, :, :, 64:65], 1.0, ["Vt1"])
    self.memset("pool", kb96[:, 0:64], 0.0, ["kb96"])
    if sample:
        self.memset("pool", kbs[:, :, 0:64], 0.0, ["kbs"])
        for j in range(3):
            S.dma("pool", halo[:, :, j], I["sc"][l, b, j].rearrange("(c p) -> p c", p=128), W=["halo"], slow=True)
        S.dma("pool", Sst[:64, :, :], I["sg"][l, b].rearrange("h k v -> k h v"), W=["S"])
    else:
        self.memset("pool", halo, 0.0, ["halo"])
        self.memset("pool", Sst[:64, :, :], 0.0, ["S"])
    gi = (l * 3 + 1) * 8
    dtb_bc, gnw_bc, qn_bc, kvn_bc = self.bc(l, "dtb"), self.bc(l, "gnw"), self.bc(l, "qn"), self.bc(l, "kvn")
    negA = self.negA[:, l * 8:(l + 1) * 8]

    def kvgen(cTt, kpTt, nk, tagk):
        for h in range(8):
            pb, pt = self.bank()
            self.mm(pb[:64, :nk], wukv[:, h * 128:h * 128 + 64], cTt[:, :nk], True, True, ["wukv", tagk], [pt])
            self.cp("act" if h % 2 else "dve", KTt[0:64, h, :nk], pb[:64, :nk], [pt], ["KTt"])
        self.cp("pool", KTt[64:96, :, :nk], kpTt[64:96, :nk].unsqueeze(1).to_broadcast([32, 8, nk]), [tagk], ["KTt"])
        wv = wukv.rearrange("p (h d) -> p h d", h=8)[:, :, 64:128]
        for kb in range((nk + 127) // 128):
            n = min(128, nk - kb * 128)
            pb, pt = self.bank()
            self.mm(pb[:n, :].rearrange("p (h d) -> p h d", h=8), cTt[:, kb * 128:kb * 128 + n], wv, True, True, ["wukv", tagk], [pt])
            self.cp("act" if kb % 2 else "dve", Vt[:n, kb, :, 0:64], pb[:n, :].rearrange("p (h d) -> p h d", h=8), [pt, "Vt1"], ["Vt"])

    def attend(nk, qblks, first, diag_kb):
        for qi, (q0, nq, kbmax) in enumerate(qblks):
            nkb = min((nk + 127) // 128, kbmax + 1)
            if nkb <= 0:
                continue
            po = [self.bank(hold=True), self.bank(hold=True)]

            def scores(kb):
                n = min(128, nk - kb * 128)
                e = eT[kb % 2]
                et = ("eT", kb % 2)
                for hb in range(2):
                    pb, pt = self.bank()
                    for hh in range(4):
                        h = hb * 4 + hh
                        self.mm(pb[:n, hh * 128:hh * 128 + nq], KTt[0:96, h, kb * 128:kb * 128 + n], QT[0:96, h, q0:q0 + nq], True, True, ["KTt", "QT"], [pt])
                    self.act(e[:n, hb * 4:hb * 4 + 4, :nq], pb[:n, :].rearrange("p (h q) -> p h q", h=4)[:, :, :nq], AF.Exp, [pt], [et], scale=SC)
                if diag_kb[qi] == kb:
                    self.memset("pool", e[64:128, :, 0:64], 0.0, [et])

            scores(0)
            for kb in range(nkb):
                if kb + 1 < nkb:
                    scores(kb + 1)
                n = min(128, nk - kb * 128)
                e = eT[kb % 2]
                et = ("eT", kb % 2)
                for h in range(8):
                    self.mm(po[h // 4][0][:nq, (h % 4) * 65:(h % 4) * 65 + 65], e[:n, h, :nq], Vt[:n, kb, h, :],
                            kb == 0 and h % 4 == 0, kb == nkb - 1 and h % 4 == 3, [et, "Vt"], [po[h // 4][1]])
            for hb in range(2):
                dst = oacc[qi][:nq, hb * 4:hb * 4 + 4, :]
                src = po[hb][0][:nq, 0:260].rearrange("p (h d) -> p h d", h=4)
                if first:
                    self.cp("dve", dst, src, [po[hb][1]], [("oacc", qi)])
                else:
                    self.tt("dve", dst, dst, src, ALU.add, [po[hb][1]], [("oacc", qi)])
            self.release(po[0][1], po[1][1])


    def mla_a(bi):
        blk = self.cur_blk0 + bi
        t0 = bi * PB
        pm, pmt = self.bank()
        for k in range(8):
            self.mm(pm[:PB, 0:416], xnT[:, k, t0:t0 + PB], wT[:, k // 2, (k % 2) * 944 + 528:(k % 2) * 944 + 944], k == 0, k == 7, ["wT", "xnT"], [pmt])
        self.cp("act", cqf[:PB, :], pm[:PB, 0:256], [pmt], ["cqf"])
        self.cp("act", cf[:PB, :], pm[:PB, 256:384], [pmt], ["cf"])
        self.cp("act", kf[:PB, :], pm[:PB, 384:416], [pmt], ["kf"])
        self.tt("dve", qf[:PB, 0:256], cqf[:PB, :], cqf[:PB, :], ALU.mult, ["cqf"], ["qf"])
        self.red(sm[:PB, 0:1], qf[:PB, 0:256], ["qf"], ["sm0"])
        self.rstd(sm[:PB, 0:1], 1.0 / QR, "sm0")
        self.stt("dve", cqn[:PB, :], cqf[:PB, :], sm[:PB, 0:1], qn_bc[:PB, :], ALU.mult, ALU.mult, ["cqf", "sm0", "const"], ["cqn"])
        cn = cnew[blk % 2]
        cnt = ("cnew", blk % 2)
        self.tt("dve", cn[:PB, :], cf[:PB, :], cf[:PB, :], ALU.mult, ["cf"], [cnt])
        self.red(sm[:PB, 1:2], cn[:PB, :], [cnt], ["sm1"])
        self.rstd(sm[:PB, 1:2], 1.0 / KVR, "sm1")
        self.stt("dve", cn[:PB, :], cf[:PB, :], sm[:PB, 1:2], kvn_bc[:PB, :], ALU.mult, ALU.mult, ["cf", "sm1", "const"], [cnt])
        S.dma("pool", o_ckv[blk * PB:(blk + 1) * PB, :], cn[:PB, :], R=[cnt], W=[])
        self.cp("act", cb[:PB, :], cn[:PB, :], [cnt], ["cb"])
        knn = kn[blk % 2]
        knt = ("kn", blk % 2)
        c1, s1 = rot[:PB, blk, 0:16], rot[:PB, blk, 16:32]
        k0, k1, k2, k3 = (rk[:PB, i_ * 16:(i_ + 1) * 16] for i_ in range(4))
        self.tt("dve", k0, kf[:PB, 0:16], c1, ALU.mult, ["kf", "rot"], ["rk0"])
        self.tt("pool", k1, kf[:PB, 16:32], s1, ALU.mult, ["kf", "rot"], ["rk1"])
        self.tt("dve", k2, kf[:PB, 16:32], c1, ALU.mult, ["kf", "rot"], ["rk2"])
        self.tt("pool", k3, kf[:PB, 0:16], s1, ALU.mult, ["kf", "rot"], ["rk3"])
        self.tt("dve", knn[:PB, 0:16], k0, k1, ALU.subtract, ["rk0", "rk1"], [knt])
        self.tt("dve", knn[:PB, 16:32], k2, k3, ALU.add, ["rk2", "rk3"], [knt])
        S.dma("pool", o_kpe[blk * PB:(blk + 1) * PB, :], knn[:PB, :], R=[knt], W=[])
        self.cp("act", kb96[:PB, 64:96], knn[:PB, :], [knt], ["kb96"])

    def mla_b1(bi):
        blk = self.cur_blk0 + bi
        t0 = bi * PB
        kc0 = t0 if sample else blk * PB
        cosb = rot[:PB, blk, 0:16].unsqueeze(1).to_broadcast([PB, 8, 16])
        sinb = rot[:PB, blk, 16:32].unsqueeze(1).to_broadcast([PB, 8, 16])
        pb, pt = self.bank()
        pbv = pb.bitcast(BF16)
        for k in range(2):
            self.tr(pbv[:, k * PB:(k + 1) * PB], cqn[:PB, k * 128:(k + 1) * 128], identb[:PB, :PB], ["cqn", "const"], [pt])
        self.cp("dve", cqnT[:, :, :PB], pbv[:, 0:2 * PB].rearrange("p (k t) -> p k t", k=2), [pt], ["cqnT"])
        pb2, pt2 = self.bank()
        pbv2 = pb2.bitcast(BF16)
        self.tr(pbv2[:, 0:PB], cb[:PB, :], identb[:PB, :PB], ["cb", "const"], [pt2])
        self.tr(pbv2[0:96, 128:128 + PB], kb96[:PB, :], identb[:PB, :PB], ["kb96", "const"], [pt2])
        self.cp("act", cTn[:, kc0:kc0 + PB], pbv2[:, 0:PB], [pt2], ["cTn"])
        self.cp("act", kpTn[64:96, kc0:kc0 + PB], pbv2[64:96, 128:128 + PB], [pt2], ["cTn"])
        q1, q1t = self.bank()
        q2, q2t = self.bank()
        for k in range(2):
            self.mm(q1[:PB, :], cqnT[:, k, :PB], wuq[:, k * 768:k * 768 + 512], k == 0, k == 1, ["cqnT", "wuq"], [q1t])
        for k in range(2):
            self.mm(q2[:PB, 0:256], cqnT[:, k, :PB], wuq[:, k * 768 + 512:k * 768 + 768], k == 0, k == 1, ["cqnT", "wuq"], [q2t])
        self.cp("act", qf[:PB, 0:512], q1[:PB, :], [q1t], ["qf"])
        self.cp("act", qf[:PB, 512:768], q2[:PB, 0:256], [q2t], ["qf"])
        qf3 = qf[:PB, :].rearrange("p (h d) -> p h d", h=8)
        qb3 = qb[:PB, :].rearrange("p (h d) -> p h d", h=8)
        x1, x2 = qf3[:, :, 64:80], qf3[:, :, 80:96]
        r0, r1, r2, r3 = (r_[:PB] for r_ in rt)
        self.tt("dve", r0, x1, cosb, ALU.mult, ["qf", "rot"], ["rt0"])
        self.tt("pool", r1, x2, sinb, ALU.mult, ["qf", "rot"], ["rt1"])
        self.tt("dve", r2, x2, cosb, ALU.mult, ["qf", "rot"], ["rt2"])
        self.tt("pool", r3, x1, sinb, ALU.mult, ["qf", "rot"], ["rt3"])
        self.tt("dve", qb3[:, :, 64:80], r0, r1, ALU.subtract, ["rt0", "rt1"], ["qb"])
        self.tt("dve", qb3[:, :, 80:96], r2, r3, ALU.add, ["rt2", "rt3"], ["qb"])
        self.cp("act", qb3[:, :, 0:64], qf3[:, :, 0:64], ["qf"], ["qb"])

    def mla_b2(bi):
        t0 = bi * PB
        qb3 = qb[:PB, :].rearrange("p (h d) -> p h d", h=8)
        pb, pt = self.bank()
        pbv = pb.bitcast(BF16)
        for h in range(8):
            self.tr(pbv[0:96, h * PB:(h + 1) * PB], qb3[:, h, :], identb[:PB, :PB], ["qb", "const"], [pt])
        self.cp("dve", QT[0:96, :, t0:t0 + PB], pbv[0:96, 0:8 * PB].rearrange("p (h t) -> p h t", h=8), [pt], ["QT"])

    for tm in range(NTM):
        blk0 = tm * BM
        self.cur_blk0 = blk0
        last = tm == NTM - 1
        self.norm_stats(blk0, BM, PB, scr[0], scr[1], xrs)
        self.norm_tr(xnT, "xnT", gi, BM, PB, xrs)
        if BM == 1 and STOP >= 3:
            mla_a(0)
        cbk = [self.bank(hold=True) for _ in range(3)] if last else None
        for j in range(6):
            w, wt = self.fetch(("F", l, j))
            for r in range(2):
                c = 2 * j + r
                pb, pt = self.bank()
                for k in range(8):
                    self.mm(pb[:, :TM], w[:, r * 1024 + k * 128:r * 1024 + (k + 1) * 128], xnT[:, k, :TM], k == 0, k == 7, [wt, "xnT"], [pt])
                if last:
                    for k in range(8):
                        self.mm(cbk[c // 4][0][0:3, (c % 4) * 128:(c % 4) * 128 + 128], xnT[:, k, TM - 3:TM],
                                w[:, r * 1024 + k * 128:r * 1024 + (k + 1) * 128], k == 0, k == 7, [wt, "xnT"], [cbk[c // 4][1]])
                rwc = rw[c % 2]
                rwt = ("rw", c % 2)
                self.cp("act", rwc[:, 0:3], halo[:, c, :], ["halo"], [rwt])
                self.cp("act", rwc[:, 3:3 + TM], pb[:, :TM], [pt], [rwt])
                self.cp("act", halo[:, c, :], rwc[:, TM:TM + 3], [rwt], ["halo"])
                cw = lambda jj, c=c: self.cwT[:, l * 48 + c * 4 + jj:l * 48 + c * 4 + jj + 1]
                self.ts("dve", cacc[:, :TM], rwc[:, 3:3 + TM], cw(3), None, ALU.mult, None, [rwt, "const"], ["cacc"])
                for jj in (2, 1, 0):
                    self.stt("dve", cacc[:, :TM], rwc[:, jj:jj + TM], cw(jj), cacc[:, :TM], ALU.mult, ALU.add, [rwt, "const", "cacc"], ["cacc"])
                cs_, cst_ = csab[c % 2], ("csa", c % 2)
                self.act(cs_[:, :TM], cacc[:, :TM], AF.Exp, ["cacc"], [cst_], scale=-1.0)
                self.ts("dve", cs_[:, :TM], cs_[:, :TM], 1.0, None, ALU.add, None, [cst_], [cst_])
                self.S.op("dve", lambda v, a=cs_[:, :TM]: v.reciprocal(out=a, in_=a), [cst_], [cst_])
                self.tt("dve", qkvT[:, c, :TM], cacc[:, :TM], cs_[:, :TM], ALU.mult, ["cacc", cst_], [("qkvT", c)])
                if c < 8:
                    self.tt("pool", csq_all[:, c * TM:(c + 1) * TM], qkvT[:, c, :TM], qkvT[:, c, :TM], ALU.mult, [("qkvT", c)], [("csq", c)])
        for c in range(8):
            p2, p2t = self.bank()
            crn = csq_all[:, c * TM:(c + 1) * TM]
            self.mm(p2[:, :TM], onesblk, crn, True, True, [("csq", c), "const"], [p2t])
            self.act(crn, p2[:, :TM], AF.Ln, [p2t], [("csq", c)], bias=EPS)
            self.act(crn, crn, AF.Exp, [("csq", c)], [("csq", c)], scale=-0.5)
            self.stt("dve", qkvT[:, c, :TM], qkvT[:, c, :TM], (0.125 if c < 4 else 1.0), crn, ALU.mult, ALU.mult, [("qkvT", c), ("csq", c)], [("qkvT", c)])

        if last:
            for k3 in range(3):
                self.cp("act", cbuf[0:3, k3 * 512:(k3 + 1) * 512], cbk[k3][0][0:3, :], [cbk[k3][1]], [("gb", 11), ("gb", 12), ("gb", 13)])
            S.dma("pool", o_conv, cbuf[0:3, :], R=[("gb", 11), ("gb", 12), ("gb", 13)], W=[])
            self.release(*[c_[1] for c_ in cbk])
        qkvR = [("qkvT", c) for c in range(12)]
        if STOP >= 3:
            if BM == 1:
                mla_b1(0)
            else:
                for bi in range(BM):
                    mla_a(bi)
                    mla_b1(bi)
                    mla_b2(bi)

        nlev = int(np.log2(C)) - 1
        g_, be_, eg_, er_, egl_, ssq_ = (gsm[:, i * 8:(i + 1) * 8] for i in range(6))

        def pe_pro(ch):
            t0 = ch * C
            pz, pzt = self.bank(hold=True)
            pab, pabt = self.bank(hold=True)
            for k in range(8):
                self.mm(pz[:C, :], xnT[:, k, t0:t0 + C], wT[:, k // 2, (k % 2) * 944:(k % 2) * 944 + 512], k == 0, k == 7, ["wT", "xnT"], [pzt])
            for k in range(8):
                self.mm(pab[:C, 0:16], xnT[:, k, t0:t0 + C], wT[:, k // 2, (k % 2) * 944 + 512:(k % 2) * 944 + 528], k == 0, k == 7, ["wT", "xnT"], [pabt])
            tb = []
            for src0 in (4, 8):
                pb, pt = self.bank(hold=True)
                for cc in range(4):
                    self.tr(pb[:C, cc * 128:(cc + 1) * 128], qkvT[:, src0 + cc, t0:t0 + C], identf, qkvR + ["const"], [pt])
                tb.append((pb, pt))
            return (pz, pzt), (pab, pabt), tb[0], tb[1]

        pro = pe_pro(0) if (NCH > 0 and STOP >= 4) else None
        if BM == 1 and STOP >= 3:
            mla_b2(0)
        og_pend = [None]
        for ch in range(NCH if STOP >= 4 else 0):
            t0 = ch * C
            (pz, pzt), (pab, pabt), (pkb, pkbt), (pvb, pvbt) = pro
            ZS, ATT, OO = 13, 11, 12
            self.act(G(ZS)[:C, :], pz[:C, :], AF.Exp, [pzt], [GT(ZS)], scale=-1.0)
            self.cp("act", gsm[:C, 48:64], pab[:C, 0:16], [pabt], ["abs"])
            self.release(pabt)
            self.ts("dve", G(ZS)[:C, :], G(ZS)[:C, :], 1.0, None, ALU.add, None, [GT(ZS)], [GT(ZS)])
            self.S.op("dve", lambda v, a=G(ZS)[:C, :]: v.reciprocal(out=a, in_=a), [GT(ZS)], [GT(ZS)])
            self.tt("dve", G(ZS)[:C, :], G(ZS)[:C, :], pz[:C, :], ALU.mult, [GT(ZS), pzt], [GT(ZS)])
            self.release(pzt)
            self.tt("dve", g_[:C, :], gsm[:C, 48:56], dtb_bc[:C, :], ALU.add, ["abs", "const"], ["g"])
            self.act(g_[:C, :], g_[:C, :], AF.Exp, ["g"], ["g"])
            self.act(g_[:C, :], g_[:C, :], AF.Ln, ["g"], ["g"], bias=1.0)
            self.tt("dve", g_[:C, :], g_[:C, :], negA[:C, :], ALU.mult, ["g", "negA"], ["g"])
            self.act(be_[:C, :], gsm[:C, 56:64], AF.Exp, ["abs"], ["beta"], scale=-1.0)
            self.ts("dve", be_[:C, :], be_[:C, :], 1.0, None, ALU.add, None, ["beta"], ["beta"])
            self.S.op("dve", lambda v, a=be_[:C, :]: v.reciprocal(out=a, in_=a), ["beta"], ["beta"])
            self.cp("act", G(9)[:C, :], pkb[:C, :], [pkbt], [GT(9)])
            self.release(pkbt)
            self.cp("act", G(10)[:C, :], pvb[:C, :], [pvbt], [GT(10)])
            self.release(pvbt)
            pq, pqt = self.bank()
            for cc in range(8):
                self.mm(pq[:64, cc * C:(cc + 1) * C], identf[:, 64:128], qkvT[:, cc, t0:t0 + C], True, True, qkvR + ["const"], [pqt])
            self.cp("act", qodd[:64, 0:8 * C], pq[:64, 0:8 * C], [pqt], ["qodd"])
            kTh = lambda h: qkvT[0:64, 4 + h // 2, t0:t0 + C] if h % 2 == 0 else qodd[:64, (4 + h // 2) * C:(5 + h // 2) * C]
            qTh = lambda h: qkvT[0:64, h // 2, t0:t0 + C] if h % 2 == 0 else qodd[:64, (h // 2) * C:(h // 2 + 1) * C]
            bbc = lambda n: be_[:C, :].unsqueeze(2).to_broadcast([C, 8, n])
            self.tt("dve", g3(0, C)[:C], strictL[:C, :C].unsqueeze(1).to_broadcast([C, 8, C]), g_[:C, :].unsqueeze(2).to_broadcast([C, 8, C]), ALU.mult, ["g", "const"], [GT(0)])
            pK, pKt = self.bank(hold=True)
            for h in range(8):
                self.mm(pK[:C, h * C:(h + 1) * C], kTh(h), kTh(h), True, True, qkvR + ["qodd"], [pKt])
            pa, pat = self.bank(hold=True)
            for h in range(8):
                self.mm(pa[:C, h * C:(h + 1) * C], kTh(h), qTh(h), True, True, qkvR + ["qodd"], [pat])
            pd, pdt = self.bank()
            self.mm(pd[:C, 0:8 * C], triU[:C, :C], G(0)[:C, 0:8 * C], True, True, [GT(0), "const"], [pdt])
            pdT, pdTt = self.bank()
            for h in range(8):
                self.mm(pdT[:C, h * C:(h + 1) * C], g3(0, C)[:C, h, :], triU[:C, :C], True, True, [GT(0), "const"], [pdTt])
            pg, pgt = self.bank()
            gb16 = gsm[:C, 0:16]
            self.mm(pg[:C, 0:16], triU[:C, :C], gb16, True, True, ["g", "beta", "const"], [pgt])
            self.mm(pg[:C, 16:32], strictL[:C, :C], gb16, True, True, ["g", "beta", "const"], [pgt])
            self.mm(pg[:64, 32:48], ones[:C, :64], gb16, True, True, ["g", "beta", "const"], [pgt])
            self.act(G(1)[:C, 0:8 * C], pd[:C, 0:8 * C], AF.Exp, [pdt], [GT(1)])
            self.act(G(2)[:C, 0:8 * C], pdT[:C, 0:8 * C], AF.Exp, [pdTt], [GT(2)])
            self.act(eg_[:C, :], pg[:C, 0:8], AF.Exp, [pgt], ["eg"])
            self.act(er_[:C, :], pg[:C, 16:24], AF.Exp, [pgt], ["eg"])
            self.act(egl_[:64, :], pg[:64, 32:40], AF.Exp, [pgt], ["egl"])
            self.tt("dve", g3(1, C)[:C], g3(1, C)[:C], strictL[:C, :C].unsqueeze(1).to_broadcast([C, 8, C]), ALU.mult, [GT(1), "const"], [GT(1)])
            self.tt("dve", g3(1, C)[:C], g3(1, C)[:C], bbc(C), ALU.mult, [GT(1), "beta"], [GT(1)])
            self.tt("dve", G(3)[:C, 0:8 * C], pK[:C, 0:8 * C], G(1)[:C, 0:8 * C], ALU.mult, [pKt, GT(1)], [GT(3)])
            self.release(pKt)
            if og_pend[0] is not None:
                og_pend[0]()
                og_pend[0] = None
            pL, pLt = self.bank()
            for h in range(8):
                self.tr(pL[:C, h * C:(h + 1) * C], g3(3, C)[:C, h, :], identf[:C, :C], [GT(3), "const"], [pLt])
            self.cp("act", G(4)[:C, 0:8 * C], pL[:C, 0:8 * C], [pLt], [GT(4)])
            self.tt("dve", g3(7, C)[:C], identf[:C, :C].unsqueeze(1).to_broadcast([C, 8, C]), g3(4, C)[:C], ALU.subtract, [GT(4), "const"], [GT(7)])
            self.tt("pool", g3(2, C)[:C], g3(2, C)[:C], triU[:C, :C].unsqueeze(1).to_broadcast([C, 8, C]), ALU.mult, [GT(2), "const"], [GT(2)])
            self.tt("dve", G(ATT)[:C, 0:8 * C], pa[:C, 0:8 * C], G(2)[:C, 0:8 * C], ALU.mult, [pat, GT(2)], [GT(ATT)])
            self.release(pat)
            egb = eg_[:C, :].unsqueeze(2).to_broadcast([C, 8, 64])
            erb = er_[:C, :].unsqueeze(2).to_broadcast([C, 8, 64])
            KBE, VB, KDEC = 0, 10, 9
            self.tt("pool", g3(KBE, 64)[:C], g3(9, 64)[:C], bbc(64), ALU.mult, [GT(9), "beta"], [GT(KBE)])
            self.tt("pool", g3(KBE, 64)[:C], g3(KBE, 64)[:C], egb, ALU.mult, [GT(KBE), "eg"], [GT(KBE)])
            self.tt("pool", g3(VB, 64)[:C], g3(VB, 64)[:C], bbc(64), ALU.mult, [GT(VB), "beta"], [GT(VB)])
            self.tt("pool", g3(KDEC, 64)[:C], g3(KDEC, 64)[:C], erb, ALU.mult, [GT(KDEC), "eg"], [GT(KDEC)])
            Pi, PTi, Ii = 3, 4, 7
            for lev in range(1, nlev + 1):
                Pn, PTn, In = (5 if Pi == 3 else 3), (6 if PTi == 4 else 4), (8 if Ii == 7 else 7)
                pP, pPt = self.bank()
                for h in range(8):
                    self.mm(pP[:C, h * C:(h + 1) * C], g3(PTi, C)[:C, h, :], g3(Pi, C)[:C, h, :], True, True, [GT(Pi), GT(PTi)], [pPt])
                if lev < nlev:
                    pQ, pQt = self.bank()
                    for h in range(8):
                        self.mm(pQ[:C, h * C:(h + 1) * C], g3(Pi, C)[:C, h, :], g3(PTi, C)[:C, h, :], True, True, [GT(Pi), GT(PTi)], [pQt])
                self.cp("act", G(Pn)[:C, 0:8 * C], pP[:C, 0:8 * C], [pPt], [GT(Pn)])
                if lev < nlev:
                    self.cp("dve", G(PTn)[:C, 0:8 * C], pQ[:C, 0:8 * C], [pQt], [GT(PTn)])
                pI, pIt = self.bank()
                for h in range(8):
                    self.mm(pI[:C, h * C:(h + 1) * C], g3(Pn, C)[:C, h, :], g3(Ii, C)[:C, h, :], True, True, [GT(Pn), GT(Ii)], [pIt])
                self.tt("dve", G(In)[:C, 0:8 * C], G(Ii)[:C, 0:8 * C], pI[:C, 0:8 * C], ALU.add, [GT(Ii), pIt], [GT(In)])
                Pi, PTi, Ii = Pn, PTn, In
            IT = Ii
            free = [i for i in (3, 4, 5, 6, 7, 8) if i != IT]
            VAL, KCT, VNEW, SQ = free[0], free[1], free[2], free[3]
            pv, pvt = self.bank()
            for h in range(8):
                self.mm(pv[:C, h * 64:(h + 1) * 64], g3(IT, C)[:C, h, :], g3(VB, 64)[:C, h, :], True, True, [GT(IT), GT(VB)], [pvt])
            pkc, pkct = self.bank()
            for h in range(8):
                self.mm(pkc[:64, h * C:(h + 1) * C], g3(KBE, 64)[:C, h, :], g3(IT, C)[:C, h, :], True, True, [GT(IT), GT(KBE)], [pkct])
            po1, po1t = self.bank()
            for h in range(8):
                self.mm(po1[:C, h * 64:(h + 1) * 64], qTh(h), Sst[:64, h, :], True, True, qkvR + ["S", "qodd"], [po1t])
            self.cp("act", G(KCT)[:64, 0:8 * C], pkc[:64, 0:8 * C], [pkct], [GT(KCT)])
            self.cp("act", G(VAL)[:C, :], pv[:C, :], [pvt], [GT(VAL)])
            pp, ppt = self.bank()
            for h in range(8):
                self.mm(pp[:C, h * 64:(h + 1) * 64], g3(KCT, C)[:64, h, :], Sst[:64, h, :], True, True, [GT(KCT), "S"], [ppt])
            self.tt("dve", G(VNEW)[:C, :], G(VAL)[:C, :], pp[:C, :], ALU.subtract, [GT(VAL), ppt], [GT(VNEW)])
            po2, po2t = self.bank()
            for h in range(8):
                self.mm(po2[:C, h * 64:(h + 1) * 64], g3(ATT, C)[:C, h, :], g3(VNEW, 64)[:C, h, :], True, True, [GT(ATT), GT(VNEW)], [po2t])
            pS, pSt = self.bank()
            for h in range(8):
                self.mm(pS[:64, h * 64:(h + 1) * 64], g3(KDEC, 64)[:C, h, :], g3(VNEW, 64)[:C, h, :], True, True, [GT(KDEC), GT(VNEW)], [pSt])
            self.tt("dve", g3(OO, 64)[:C], po1[:C, :].rearrange("p (h d) -> p h d", h=8), egb, ALU.mult, [po1t, "eg"], [GT(OO)])
            self.tt("dve", G(OO)[:C, :], G(OO)[:C, :], po2[:C, :], ALU.add, [GT(OO), po2t], [GT(OO)])
            self.tt("dve", G(SQ)[:C, :], G(OO)[:C, :], G(OO)[:C, :], ALU.mult, [GT(OO)], [GT(SQ)])
            self.red(ssq_[:C, :], g3(SQ, 64)[:C], [GT(SQ)], ["ssq"])
            self.rstd(ssq_[:C, :], 1.0 / 64, "ssq")
            self.tt("dve", G(ZS)[:C, :].rearrange("p (h d) -> p h d", h=8), G(ZS)[:C, :].rearrange("p (h d) -> p h d", h=8),
                    gnw_bc[:C, :].unsqueeze(1).to_broadcast([C, 8, 64]), ALU.mult, [GT(ZS), "const"], [GT(ZS)])
            self.tt("dve", g3(OO, 64)[:C], g3(OO, 64)[:C], ssq_[:C, :].unsqueeze(2).to_broadcast([C, 8, 64]), ALU.mult, [GT(OO), "ssq"], [GT(OO)])
            self.tt("dve", mog[:C, :], G(OO)[:C, :], G(ZS)[:C, :], ALU.mult, [GT(OO), GT(ZS)], ["mog"])
            self.tt("pool", Sst[:64], Sst[:64], egl_[:64, :].unsqueeze(2).to_broadcast([64, 8, 64]), ALU.mult, ["S", "egl"], ["S"])
            self.tt("dve", Sst[:64], Sst[:64], pS[:64, :].rearrange("p (h d) -> p h d", h=8), ALU.add, ["S", pSt], ["S"])
            if ch + 1 < NCH:
                pro = pe_pro(ch + 1)
            def og_emit(t0=t0):
                pb, pt = self.bank()
                pbv = pb.bitcast(BF16)
                for cc in range(4):
                    self.tr(pbv[:, cc * C:(cc + 1) * C], mog[:C, cc * 128:(cc + 1) * 128], identb[:C, :C], ["mog", "const"], [pt])
                self.cp("act", mixT[:, 0:4, t0:t0 + C], pbv[:, 0:4 * C].rearrange("p (k t) -> p k t", k=4), [pt], ["mixT"])
            og_pend[0] = og_emit

        if STOP < 5:
            continue
        if not sample:
            qblks_all = []
            for kt in range((blk0 + BM - 1) // 4 + 1):
                nk = min(512, (blk0 + BM) * PB - kt * 512)
                kvgen(cT[:, kt * 512:kt * 512 + nk], kpT[:, kt * 512:kt * 512 + nk], nk, "cTn")
                qblks = []
                diag = []
                for bi in range(BM):
                    gq = blk0 + bi
                    kbmax = gq - kt * 4
                    qblks.append((bi * PB, PB, min(kbmax, 3)))
                    diag.append(kbmax if 0 <= kbmax <= 3 else None)
                attend(nk, qblks, kt == 0, diag)
        else:
            npt = PAST // 512
            for kt in range(npt):
                S.dma("pool", cst, I["cckv"][l, b, kt * 512:(kt + 1) * 512, :].rearrange("(k p) d -> p k d", p=128), W=["cst"])
                S.dma("pool", kst, I["ckr"][l, b, kt * 512:(kt + 1) * 512, :].rearrange("(k p) d -> p k d", p=128), W=["kst"])
                self.cp("act", cbs, cst, ["cst"], ["cbs"])
                self.cp("dve", kbs[:, :, 64:96], kst, ["kst", "kbs"], ["kbs"])
                pb, pt = self.bank()
                pbv = pb.bitcast(BF16)
                for k in range(4):
                    self.tr(pbv[:, k * 128:(k + 1) * 128], cbs[:, k, :], identb, ["cbs", "const"], [pt])
                self.cp("dve", cT[:, 0:512], pbv[:, 0:512], [pt], ["cT"])
                pb, pt = self.bank()
                pbv = pb.bitcast(BF16)
                for k in range(4):
                    self.tr(pbv[0:96, k * 128:(k + 1) * 128], kbs[:, k, :], identb, ["kbs", "const"], [pt])
                self.cp("dve", kpT[64:96, 0:512], pbv[64:96, 0:512], [pt], ["cT"])
                kvgen(cT[:, 0:512], kpT[:, 0:512], 512, "cT")
                attend(512, [(0, T, 3)], kt == 0, [None])
            kvgen(cTn[:, 0:T], kpTn[:, 0:T], T, "cTn")
            attend(T, [(0, T, 0)], npt == 0, [None])
        if og_pend[0] is not None:
            og_pend[0]()
            og_pend[0] = None
        for bi in range(BM):
            self.S.op("dve", lambda v, o=sm[:PB, 8:16], i=oacc[bi][:PB, :, 64]: v.reciprocal(out=o, in_=i), [("oacc", bi)], ["sm8"])
            self.tt("dve", mo[:PB, :].rearrange("p (h d) -> p h d", h=8), oacc[bi][:PB, :, 0:64], sm[:PB, 8:16].unsqueeze(2).to_broadcast([PB, 8, 64]),
                    ALU.mult, [("oacc", bi), "sm8"], ["mo"])
            pb, pt = self.bank()
            pbv = pb.bitcast(BF16)
            for cc in range(4):
                self.tr(pbv[:, cc * PB:(cc + 1) * PB], mo[:PB, cc * 128:(cc + 1) * 128], identb[:PB, :PB], ["mo", "const"], [pt])
            self.cp("act", mixT[:, 4:8, bi * PB:(bi + 1) * PB], pbv[:, 0:4 * PB].rearrange("p (k t) -> p k t", k=4), [pt], ["mixT"])
        for half in range(2):
            banks = [self.bank(hold=True) for _ in range(BM)]
            for j in range(4):
                w, wt = self.fetch(("O", l, j))
                for r in range(2):
                    kc = 2 * j + r
                    for bi in range(BM):
                        self.mm(banks[bi][0][:PB, :], mixT[:, kc, bi * PB:(bi + 1) * PB], w[:, r * 1024 + half * 512:r * 1024 + half * 512 + 512],
                                kc == 0, kc == 7, [wt, "mixT"], [banks[bi][1]])
            for bi in range(BM):
                xs = self.X[:PB, blk0 + bi, half * 512:(half + 1) * 512]
                self.tt("dve", xs, xs, banks[bi][0][:PB, :], ALU.add, [banks[bi][1]], [("X", blk0 + bi)])
                self.release(banks[bi][1])
    S.dma("pool", o_gdn.rearrange("h k v -> k h v"), Sst[:64, :, :], R=["S"], W=[])
    S.barrier()
    A.reset(m0)


def _consts(C_p, C_s):
    out = np.zeros((2, 6, 128, 128), np.float32)
    idx = np.arange(128)
    for s, Cc in enumerate((C_p, C_s)):
        same = (idx[:, None] // Cc) == (idx[None, :] // Cc)
        out[s, 0] = ((idx[:, None] <= idx[None, :]) & same)
        out[s, 1] = ((idx[:, None] > idx[None, :]) & same)
        out[s, 2] = ((idx[:, None] < idx[None, :]) & same)
        out[s, 3] = (idx[:, None] // 64) == (idx[None, :] // 64)
        out[s, 4] = 1.0
        out[s, 5] = np.eye(128)
    return out


def _rot_table(pos):
    half = ROPE // 2
    inv_freq = (1.0 / (np.float32(10000.0) ** (np.arange(half, dtype=np.float32) / np.float32(half)))).astype(np.float32)
    ang = pos.astype(np.float32)[:, None] * inv_freq[None, :]
    return np.concatenate([np.cos(ang), np.sin(ang)], axis=1).astype(np.float32)


_CACHE = {}


def _get_program(shp):
    key = tuple(sorted(shp.items()))
    if key in _CACHE:
        return _CACHE[key]
    nc0 = bass.Bass("TRN2", target_bir_lowering=False)
    b0 = Builder(nc0, shp, dry=True)
    b0.build()
    plan = b0.req
    nc = bass.Bass("TRN2", target_bir_lowering=False)
    bld = Builder(nc, shp, dry=False, plan=plan)
    bld.build()
    S = bld.S
    with nc.Block() as block:
        @block.tensor
        def _(e):
            S.replay("pe", e)

        @block.scalar
        def _(e):
            S.replay("act", e)

        @block.vector
        def _(e):
            S.replay("dve", e)

        @block.gpsimd
        def _(e):
            S.replay("pool", e)

        @block.sync
        def _(e):
            S.replay("sp", e)
    _CACHE[key] = nc
    return nc


def kernel(x_prompt, x_sample, cache_mla_ckv, cache_mla_krope, state_gdn, state_gdn_conv,
           norm_ffn1, w_ffn1_gate, w_ffn1_up, w_ffn1_down, norm_mix, w_in, gdn_conv_w, gdn_a_log,
           gdn_dt_bias, gdn_norm_w, mla_q_norm, mla_kv_norm, w_uq, w_ukv, w_out, norm_ffn2,
           w_ffn2_gate, w_ffn2_up, w_ffn2_down, norm_final):
    f = lambda a: np.ascontiguousarray(np.asarray(a, dtype=np.float32))
    x_prompt, x_sample = f(x_prompt), f(x_sample)
    B, T, _ = x_prompt.shape
    BS, TS, _ = x_sample.shape
    depth = w_in.shape[0]
    PAST = cache_mla_ckv.shape[2]
    NP, NS = B // NCORES, BS // NCORES
    shp = dict(depth=depth, NP=NP, T=T, NS=NS, TS=TS, PAST=PAST)
    nc = _get_program(shp)
    C_p = 64 if T % 64 == 0 else T
    C_s = 64 if TS % 64 == 0 else TS
    common = dict(
        nf1=f(norm_ffn1), wg1=f(w_ffn1_gate), wu1=f(w_ffn1_up), wd1=f(w_ffn1_down),
        nf2=f(norm_ffn2), wg2=f(w_ffn2_gate), wu2=f(w_ffn2_up), wd2=f(w_ffn2_down),
        nm=f(norm_mix), win=f(w_in), cw=f(gdn_conv_w), alog=f(gdn_a_log), dtb=f(gdn_dt_bias),
        gnw=f(gdn_norm_w), qn=f(mla_q_norm), kvn=f(mla_kv_norm), wuq=f(w_uq), wukv=f(w_ukv),
        wo=f(w_out), nfin=f(norm_final).reshape(1, D),
        rotp=_rot_table(np.arange(T)), rots=_rot_table(PAST + np.arange(TS)),
        cmask=_consts(C_p, C_s), identb=np.eye(128, dtype=np.float32).astype(ml_dtypes.bfloat16),
    )
    ckv, ckr, sg, sc = f(cache_mla_ckv), f(cache_mla_krope), f(state_gdn), f(state_gdn_conv)
    in_maps = []
    for c in range(NCORES):
        m = dict(common)
        m["xp"] = x_prompt[c * NP:(c + 1) * NP]
        m["xs"] = x_sample[c * NS:(c + 1) * NS]
        m["cckv"] = np.ascontiguousarray(ckv[:, c * NS:(c + 1) * NS])
        m["ckr"] = np.ascontiguousarray(ckr[:, c * NS:(c + 1) * NS])
        m["sg"] = np.ascontiguousarray(sg[:, c * NS:(c + 1) * NS])
        m["sc"] = np.ascontiguousarray(sc[:, c * NS:(c + 1) * NS])
        in_maps.append(m)
    res = run_bass_kernel_spmd(nc, in_maps, core_ids=list(range(NCORES)))
    R = res.results
    cat = lambda k, ax: np.concatenate([np.asarray(r[k], dtype=np.float32) for r in R], axis=ax)
    return (cat("yp", 0), cat("ys", 0), cat("pckv", 1), cat("pkpe", 1), cat("pgdn", 1), cat("pconv", 1),
            cat("sckv", 1), cat("skpe", 1), cat("sgdn", 1), cat("sconv", 1))
```

```python
import os
import numpy as np
import ml_dtypes
import concourse.bass as bass
import concourse.mybir as mybir
from concourse.bass_utils import run_bass_kernel_spmd

F32 = mybir.dt.float32
BF16 = mybir.dt.bfloat16
AF = mybir.ActivationFunctionType
ALU = mybir.AluOpType
AX = mybir.AxisListType

NCORES = 8
D = 1024
DFF = 2816
NM = DFF // 128
H = 8
DK = 64
QR = 256
KVR = 128
ROPE = 32
NOPE = 64
INC = 2480
CONVD = 1536
EPS = 1e-6
SLOT = 2048
RING = 4
NDMA = 6
STOP = int(os.environ.get('DBG_STOP', '9'))
GSTOP = float(os.environ.get('DBG_GSTOP', '9'))


class Sched:
    ENG = ("pe", "act", "dve", "pool", "sp")

    def __init__(self, nc, dry):
        self.nc = nc
        self.dry = dry
        self.cnt = {e: 0 for e in self.ENG}
        self.seen = {e: {} for e in self.ENG}
        self.prog = {e: [] for e in self.ENG}
        self.tok_w = {}
        self.tok_r = {}
        self.dma_val = {}
        self.dma_rr = {"sp": 0, "pool": 0}
        self.sems = {}
        if not dry:
            for e in self.ENG:
                self.sems[e] = nc.alloc_semaphore(name=f"sq_{e}")
            for q in ("sp", "pool"):
                for k in range(NDMA):
                    self.sems[("d", q, k)] = nc.alloc_semaphore(name=f"sd_{q}_{k}")

    def _deps(self, e, R, W):
        waits = {}

        def need(dep):
            if dep is None:
                return
            sk, val = dep
            if sk == e and e == "pe":
                return
            if self.seen[e].get(sk, 0) >= val:
                return
            if waits.get(sk, 0) < val:
                waits[sk] = val

        for t in R:
            need(self.tok_w.get(t))
        for t in W:
            need(self.tok_w.get(t))
            for d in self.tok_r.get(t, ()):
                need(d)
        return waits

    def _commit(self, me, R, W):
        for t in R:
            self.tok_r.setdefault(t, []).append(me)
        for t in W:
            self.tok_w[t] = me
            self.tok_r[t] = []

    def op(self, e, fn, R=(), W=()):
        if self.dry:
            return
        waits = self._deps(e, R, W)
        for sk, v in waits.items():
            self.seen[e][sk] = v
        self.cnt[e] += 1
        me = (e, self.cnt[e])
        self.seen[e][e] = max(self.seen[e].get(e, 0), 0)
        self.prog[e].append((list(waits.items()), fn, (e, 1)))
        self._commit(me, R, W)

    def dma(self, q, out, in_, R=(), W=(), slow=False):
        if self.dry:
            return
        k = self.dma_rr[q]
        self.dma_rr[q] = (k + 1) % NDMA
        sk = ("d", q, k)
        prev = self.dma_val.get(sk, 0)
        waits = self._deps(q, R, W)
        if prev > 0 and self.seen[q].get(sk, 0) < prev:
            waits[sk] = max(waits.get(sk, 0), prev)
        for s2, v in waits.items():
            self.seen[q][s2] = v
        self.dma_val[sk] = prev + 16
        me = (sk, prev + 16)
        fn = (lambda eng, o=out, i=in_: eng.dma_start(out=o, in_=i, allow_slow_non_contiguous=True)) if slow else (lambda eng, o=out, i=in_: eng.dma_start(out=o, in_=i))
        self.prog[q].append((list(waits.items()), fn, (sk, 16)))
        self._commit(me, R, W)

    def barrier(self, engines=("pe", "act", "dve", "pool")):
        if self.dry:
            return
        for e in engines:
            waits = {}
            for o in engines:
                if o != e and self.cnt[o] > self.seen[e].get(o, 0):
                    waits[o] = self.cnt[o]
            for sk, v in self.dma_val.items():
                if self.seen[e].get(sk, 0) < v:
                    waits[sk] = v
            for s2, v in waits.items():
                self.seen[e][s2] = v
            if waits:
                self.prog[e].append((list(waits.items()), None, None))
        waits = {}
        for o in engines:
            if self.cnt[o] > self.seen["sp"].get(o, 0):
                waits[o] = self.cnt[o]
        for s2, v in waits.items():
            self.seen["sp"][s2] = v
        if waits:
            self.prog["sp"].append((list(waits.items()), None, None))

    def finish(self):
        waits = {}
        for sk, v in self.dma_val.items():
            waits[sk] = v
        for e in ("pe", "act", "dve", "pool"):
            waits[e] = self.cnt[e]
        self.prog["sp"].append((list(waits.items()), None, None))
        self.prog["pool"].append((list(waits.items()), None, None))

    def replay(self, e, eng):
        for waits, fn, inc in self.prog[e]:
            for sk, v in waits:
                if v > 0:
                    eng.wait_ge(self.sems[sk], v)
            if fn is not None:
                ins = fn(eng)
                ins.then_inc(self.sems[inc[0]], inc[1])


class Arena:
    def __init__(self, ap_f32):
        self.ap = ap_f32
        self.n = ap_f32.shape[1]
        self.pos = 0

    def alloc(self, cols, dtype=F32):
        w = cols if dtype == F32 else (cols + 1) // 2
        w = (w + 7) // 8 * 8
        assert self.pos + w <= self.n, ("arena overflow", self.pos, w, self.n)
        v = self.ap[:, self.pos:self.pos + w]
        self.pos += w
        if dtype != F32:
            v = v.bitcast(dtype)[:, :cols]
        else:
            v = v[:, :cols]
        return v

    def mark(self):
        return self.pos

    def reset(self, m):
        self.pos = m


def _slot_plan(depth):
    keys = []
    for l in range(depth):
        for f in (1, 2):
            for m in range(NM):
                keys.append(("A", l, f, m))
            for half in range(2):
                for j in range(6):
                    keys.append(("B", l, f, half, j))
        for j in range(6):
            keys.append(("F", l, j))
        for j in range(4):
            keys.append(("T", l, j))
        keys.append(("UQ", l))
        keys.append(("UKV", l))
        for j in range(4):
            keys.append(("O", l, j))
    return keys


class Builder:
    def __init__(self, nc, shp, dry, plan=None):
        self.nc = nc
        self.shp = shp
        self.dry = dry
        self.S = Sched(nc, dry)
        self.plan = plan
        self.req = []
        self.pos = 0
        self.issued = 0
        self.bank_rr = 0
        self.held = set()

    def mm(self, out, lhsT, rhs, start, stop, R, W):
        self.S.op("pe", lambda t, o=out, a=lhsT, b=rhs, s=start, p=stop: t.matmul(o, lhsT=a, rhs=b, start=s, stop=p), R, W)

    def tr(self, out, in_, ident, R, W):
        self.S.op("pe", lambda t, o=out, a=in_, b=ident: t.transpose(o, a, b), R, W)

    def act(self, out, in_, func, R, W, bias=None, scale=None):
        kw = {}
        if bias is not None:
            kw["bias"] = bias
        if scale is not None:
            kw["scale"] = scale
        self.S.op("act", lambda a, o=out, i=in_, f=func, k=kw: a.activation(out=o, in_=i, func=f, **k), R, W)

    def tt(self, e, out, in0, in1, op, R, W):
        self.S.op(e, lambda v, o=out, a=in0, b=in1, p=op: v.tensor_tensor(out=o, in0=a, in1=b, op=p), R, W)

    def ts(self, e, out, in0, s1, s2, op0, op1, R, W):
        if s2 is None:
            self.S.op(e, lambda v, o=out, a=in0, x=s1, p0=op0: v.tensor_scalar(out=o, in0=a, scalar1=x, scalar2=None, op0=p0), R, W)
        else:
            self.S.op(e, lambda v, o=out, a=in0, x=s1, y=s2, p0=op0, p1=op1: v.tensor_scalar(out=o, in0=a, scalar1=x, scalar2=y, op0=p0, op1=p1), R, W)

    def stt(self, e, out, in0, scalar, in1, op0, op1, R, W):
        self.S.op(e, lambda v, o=out, a=in0, s=scalar, b=in1, p0=op0, p1=op1: v.scalar_tensor_tensor(out=o, in0=a, scalar=s, in1=b, op0=p0, op1=p1), R, W)

    def cp(self, e, out, in_, R, W):
        if e == "act":
            self.S.op("act", lambda a, o=out, i=in_: a.copy(out=o, in_=i), R, W)
        else:
            self.S.op(e, lambda v, o=out, i=in_: v.tensor_copy(out=o, in_=i), R, W)

    def red(self, out, in_, R, W):
        self.S.op("dve", lambda v, o=out, i=in_: v.tensor_reduce(out=o, in_=i, axis=AX.X, op=ALU.add), R, W)

    def rstd(self, ap, scale, tok):
        self.act(ap, ap, AF.Ln, [tok], [tok], bias=EPS, scale=scale)
        self.act(ap, ap, AF.Exp, [tok], [tok], scale=-0.5)

    def memset(self, e, ap, val, W):
        self.S.op(e, lambda v, a=ap, c=val: v.memset(a, c), (), W)

    def bank(self, hold=False):
        i = self.bank_rr
        while i in self.held:
            i = (i + 1) % 8
        self.bank_rr = (i + 1) % 8
        if hold:
            self.held.add(i)
        return self.ps[i], ("ps", i)

    def release(self, *toks):
        for t in toks:
            self.held.discard(t[1])

    def fetch(self, key):
        if self.dry:
            self.req.append(key)
            return self.wring[:, 0, :], ("wr", 0)
        i = self.pos
        assert self.plan[i] == key, (i, self.plan[i], key)
        last = min(i + RING - 1, len(self.plan) - 1)
        while self.issued <= last:
            j = self.issued
            sid = self.slot_id[self.plan[j]]
            self.S.dma("sp", self.wring[:, j % RING, :], self.wscr[sid], R=[("wscr", sid)], W=[("wr", j % RING)])
            self.issued += 1
        self.pos += 1
        return self.wring[:, i % RING, :], ("wr", i % RING)

    def build(self):
        nc, shp = self.nc, self.shp
        depth, NP, T, NS, TS, PAST = shp["depth"], shp["NP"], shp["T"], shp["NS"], shp["TS"], shp["PAST"]
        self.depth = depth
        dt = lambda name, shape, dtype=F32, kind="ExternalInput": nc.dram_tensor(name, list(shape), dtype, kind=kind).ap()
        I = {}
        I["xp"] = dt("xp", [NP, T, D]); I["xs"] = dt("xs", [NS, TS, D])
        I["cckv"] = dt("cckv", [depth, NS, PAST, KVR]); I["ckr"] = dt("ckr", [depth, NS, PAST, ROPE])
        I["sg"] = dt("sg", [depth, NS, H, DK, DK]); I["sc"] = dt("sc", [depth, NS, 3, CONVD])
        for f in (1, 2):
            I[f"nf{f}"] = dt(f"nf{f}", [depth, D]); I[f"wg{f}"] = dt(f"wg{f}", [depth, D, DFF])
            I[f"wu{f}"] = dt(f"wu{f}", [depth, D, DFF]); I[f"wd{f}"] = dt(f"wd{f}", [depth, DFF, D])
        I["nm"] = dt("nm", [depth, D]); I["win"] = dt("win", [depth, D, INC]); I["cw"] = dt("cw", [depth, 4, CONVD])
        I["alog"] = dt("alog", [depth, H]); I["dtb"] = dt("dtb", [depth, H]); I["gnw"] = dt("gnw", [depth, DK])
        I["qn"] = dt("qn", [depth, QR]); I["kvn"] = dt("kvn", [depth, KVR]); I["wuq"] = dt("wuq", [depth, QR, H * 96])
        I["wukv"] = dt("wukv", [depth, KVR, H * 128]); I["wo"] = dt("wo", [depth, D, D]); I["nfin"] = dt("nfin", [1, D])
        I["rotp"] = dt("rotp", [T, 32]); I["rots"] = dt("rots", [TS, 32])
        I["cmask"] = dt("cmask", [2, 6, 128, 128]); I["identb"] = dt("identb", [128, 128], BF16)
        O = {}
        ko = "ExternalOutput"
        O["yp"] = dt("yp", [NP, T, D], kind=ko); O["ys"] = dt("ys", [NS, TS, D], kind=ko)
        O["pckv"] = dt("pckv", [depth, NP, T, KVR], kind=ko); O["pkpe"] = dt("pkpe", [depth, NP, T, ROPE], kind=ko)
        O["pgdn"] = dt("pgdn", [depth, NP, H, DK, DK], kind=ko); O["pconv"] = dt("pconv", [depth, NP, 3, CONVD], kind=ko)
        O["sckv"] = dt("sckv", [depth, NS, TS, KVR], kind=ko); O["skpe"] = dt("skpe", [depth, NS, TS, ROPE], kind=ko)
        O["sgdn"] = dt("sgdn", [depth, NS, H, DK, DK], kind=ko); O["sconv"] = dt("sconv", [depth, NS, 3, CONVD], kind=ko)
        self.I, self.O = I, O
        keys = _slot_plan(depth)
        self.slot_id = {k: i for i, k in enumerate(keys)}
        self.wscr = nc.dram_tensor("wscr", [len(keys), 128, SLOT], BF16, kind="Internal").ap()

        NBLK = max(T // 128, 1)
        self.ps = [nc.alloc_psum_tensor(f"psb{i}", [128, 512], F32)[:] for i in range(8)]
        xall = nc.alloc_sbuf_tensor("xres", [128, NBLK * D], F32)[:]
        self.X = xall.rearrange("p (b d) -> p b d", d=D)
        self.wring = nc.alloc_sbuf_tensor("wring", [128, RING * SLOT], BF16)[:].rearrange("p (r s) -> p r s", s=SLOT)
        avail = (nc.sbuf_top - nc.sbuf_base) - 2048
        ar = nc.alloc_sbuf_tensor("arena", [128, avail // 4], F32)[:]
        self.A = Arena(ar)
        A = self.A
        self.cm = A.alloc(2 * 6 * 128).rearrange("p (s k c) -> p s k c", s=2, k=6)
        self.identb = A.alloc(128, BF16)
        self.identf = self.cm[:, 0, 5, :]
        self.gT = A.alloc(depth * 3 * 8)
        self.cwT = A.alloc(depth * 12 * 4)
        self.smallbc = A.alloc(depth * (8 + 8 + 64 + QR + KVR))
        self.negA = A.alloc(depth * 8)
        S = self.S
        if True:
            S.dma("pool", self.cm, I["cmask"].rearrange("s k p c -> p s k c"), W=["const"])
            S.dma("pool", self.identb, I["identb"], W=["const"])
            for l in range(depth):
                for wi, nm in enumerate(("nf1", "nm", "nf2")):
                    o = (l * 3 + wi) * 8
                    S.dma("pool", self.gT[:, o:o + 8], I[nm][l].rearrange("(k p) -> p k", p=128), W=["const"], slow=True)
                for j in range(4):
                    S.dma("pool", self.cwT[:, l * 48:(l + 1) * 48].rearrange("p (c j) -> p c j", j=4)[:, :, j],
                          I["cw"][l][j].rearrange("(c p) -> p c", p=128), W=["const"], slow=True)
                o = l * (16 + 64 + QR + KVR)
                sb = self.smallbc
                S.dma("pool", sb[:, o:o + 8], I["alog"][l:l + 1, :].partition_broadcast(128), W=["const"])
                S.dma("pool", sb[:, o + 8:o + 16], I["dtb"][l:l + 1, :].partition_broadcast(128), W=["const"])
                S.dma("pool", sb[:, o + 16:o + 80], I["gnw"][l:l + 1, :].partition_broadcast(128), W=["const"])
                S.dma("pool", sb[:, o + 80:o + 80 + QR], I["qn"][l:l + 1, :].partition_broadcast(128), W=["const"])
                S.dma("pool", sb[:, o + 80 + QR:o + 80 + QR + KVR], I["kvn"][l:l + 1, :].partition_broadcast(128), W=["const"])
        for l in range(depth):
            o = l * (16 + 64 + QR + KVR)
            self.act(self.negA[:, l * 8:(l + 1) * 8], self.smallbc[:, o:o + 8], AF.Exp, ["const"], ["negA"])
        self.ts("dve", self.negA, self.negA, -1.0, None, ALU.mult, None, ["negA"], ["negA"])

        self.prepass()
        base = A.mark()
        for b in range(NP):
            A.reset(base)
            self.sequence(b, T, False)
        for b in range(NS):
            A.reset(base)
            self.sequence(b, TS, True)
        S.finish()

    def bc(self, l, which):
        o = l * (16 + 64 + QR + KVR)
        off, n = {"alog": (0, 8), "dtb": (8, 8), "gnw": (16, 64), "qn": (80, QR), "kvn": (80 + QR, KVR)}[which]
        return self.smallbc[:, o + off:o + off + n]

    def prepass(self):
        S, I, A = self.S, self.I, self.A
        m0 = A.mark()
        stg = [A.alloc(SLOT, BF16) for _ in range(3)]
        n = 0
        for key, sid in self.slot_id.items():
            st = stg[n % 3]
            tok = ("stg", n % 3)
            n += 1
            parts = []
            kind, l = key[0], key[1]
            if kind == "A":
                f, m = key[2], key[3]
                for r, wn in enumerate(("wg", "wu")):
                    parts.append((st[:, r * 1024:(r + 1) * 1024].rearrange("p (k c) -> p k c", c=128),
                                  I[f"{wn}{f}"][l][:, m * 128:(m + 1) * 128].rearrange("(k p) c -> p k c", p=128)))
            elif kind == "B":
                f, half, j = key[2], key[3], key[4]
                nr = min(4, NM - 4 * j)
                parts.append((st[:, 0:nr * 512].rearrange("p (r n) -> p r n", n=512),
                              I[f"wd{f}"][l][4 * j * 128:(4 * j + nr) * 128, half * 512:(half + 1) * 512].rearrange("(r p) n -> p r n", p=128)))
            elif kind == "F":
                j = key[2]
                for r in range(2):
                    c = 2 * j + r
                    parts.append((st[:, r * 1024:(r + 1) * 1024].rearrange("p (k c) -> p k c", c=128),
                                  I["win"][l][:, c * 128:(c + 1) * 128].rearrange("(k p) c -> p k c", p=128)))
            elif kind == "T":
                j = key[2]
                parts.append((st[:, 0:2 * 944].rearrange("p (r n) -> p r n", n=944),
                              I["win"][l][2 * j * 128:(2 * j + 2) * 128, CONVD:INC].rearrange("(r p) n -> p r n", p=128)))
            elif kind == "UQ":
                parts.append((st[:, 0:2 * 768].rearrange("p (r n) -> p r n", n=768),
                              I["wuq"][l].rearrange("(r p) n -> p r n", p=128)))
            elif kind == "UKV":
                parts.append((st[:, 0:1024], I["wukv"][l]))
            elif kind == "O":
                j = key[2]
                parts.append((st[:, 0:2048].rearrange("p (r n) -> p r n", n=1024),
                              I["wo"][l][2 * j * 128:(2 * j + 2) * 128, :].rearrange("(r p) n -> p r n", p=128)))
            for o_, i_ in parts:
                S.dma("pool", o_, i_, W=[tok])
            S.dma("sp", self.wscr[sid], st, R=[tok], W=[("wscr", sid)])
        A.reset(m0)

    def norm_stats(self, blk0, nblk, PB, sq, ss, xrs):
        for b in range(nblk):
            xb = self.X[:PB, blk0 + b, :]
            xt = ("X", blk0 + b)
            self.tt("dve", sq[:PB, :], xb, xb, ALU.mult, [xt], ["nsq"])
            self.red(ss[:PB, b:b + 1], sq[:PB, :], ["nsq"], [("nss", b)])
            self.rstd(ss[:PB, b:b + 1], 1.0 / D, ("nss", b))
            self.ts("dve", xrs[b][:PB, :], xb, ss[:PB, b:b + 1], None, ALU.mult, None, [xt, ("nss", b)], [("nxr", b)])

    def norm_tr(self, xnT, xtok, gi, nblk, PB, xrs):
        for b in range(nblk):
            pb, pt = self.bank()
            pbv = pb.bitcast(BF16)
            for k in range(8):
                self.tr(pbv[:, k * PB:(k + 1) * PB], xrs[b][:PB, k * 128:(k + 1) * 128], self.identb[:PB, :PB], [("nxr", b), "const"], [pt])
            self.tt("dve", xnT[:, :, b * PB:(b + 1) * PB], pbv[:, 0:8 * PB].rearrange("p (k t) -> p k t", t=PB),
                    self.gT[:, gi:gi + 8].unsqueeze(2).to_broadcast([128, 8, PB]), ALU.mult, [pt, "const"], [xtok])

    def ffn_A(self, l, f, nblk, PB, xnT, xtok, hT):
        TT = nblk * PB
        for m in range(NM):
            w, wt = self.fetch(("A", l, f, m))
            pg, pgt = self.bank()
            pu, put = self.bank()
            for k in range(8):
                self.mm(pg[:, :TT], w[:, k * 128:(k + 1) * 128], xnT[:, k, :TT], k == 0, k == 7, [wt, xtok], [pgt])
            for k in range(8):
                self.mm(pu[:, :TT], w[:, 1024 + k * 128:1024 + (k + 1) * 128], xnT[:, k, :TT], k == 0, k == 7, [wt, xtok], [put])
            sg = self.sgb[m % 2]
            self.act(sg[:, :TT], pg[:, :TT], AF.Silu, [pgt], [("sg", m % 2)])
            self.tt("dve", hT[:, m, :TT], sg[:, :TT], pu[:, :TT], ALU.mult, [("sg", m % 2), put], [("hT", m)])

    def ffn_B(self, l, f, blk0, nblk, PB, hT):
        for half in range(2):
            banks = [self.bank(hold=True) for _ in range(nblk)]
            for j in range(6):
                w, wt = self.fetch(("B", l, f, half, j))
                for r in range(min(4, NM - 4 * j)):
                    m = 4 * j + r
                    for b in range(nblk):
                        self.mm(banks[b][0][:PB, :], hT[:, m, b * PB:(b + 1) * PB], w[:, r * 512:(r + 1) * 512],
                                m == 0, m == NM - 1, [wt, ("hT", m)], [banks[b][1]])
            for b in range(nblk):
                xs = self.X[:PB, blk0 + b, half * 512:(half + 1) * 512]
                self.stt("dve", xs, banks[b][0][:PB, :], 0.5, xs, ALU.mult, ALU.add, [banks[b][1]], [("X", blk0 + b)])
                self.release(banks[b][1])


    def sequence(self, b, T, sample):
        S, I, O, A = self.S, self.I, self.O, self.A
        PB = min(128, T)
        NBLK = T // PB
        BPT = min(4, NBLK)
        NT = NBLK // BPT
        TT = BPT * PB
        C = 64 if T % 64 == 0 else T
        xin = (I["xs"] if sample else I["xp"])[b]
        for blk in range(NBLK):
            S.dma("pool", self.X[:PB, blk, :], xin[blk * PB:(blk + 1) * PB, :], W=[("X", blk)])
        for l in range(self.depth):
            self.ffn_phase(l, 1, NT, BPT, PB)
            self.mixer_phase(l, b, T, sample, NT, BPT, PB, C)
            self.ffn_phase(l, 2, NT, BPT, PB)
        S.barrier()
        m0 = A.mark()
        sq = A.alloc(D); ss = A.alloc(8); yo = [A.alloc(D) for _ in range(2)]
        gfin = A.alloc(D)
        S.dma("pool", gfin, I["nfin"][0:1, :].partition_broadcast(128), W=["gfin"])
        yout = (O["ys"] if sample else O["yp"])[b]
        for blk in range(NBLK):
            xb = self.X[:PB, blk, :]
            xt = ("X", blk)
            self.tt("dve", sq[:PB, :], xb, xb, ALU.mult, [xt], ["nsq"])
            self.red(ss[:PB, 0:1], sq[:PB, :], ["nsq"], ["nss"])
            self.rstd(ss[:PB, 0:1], 1.0 / D, "nss")
            y = yo[blk % 2]
            self.stt("dve", y[:PB, :], xb, ss[:PB, 0:1], gfin[:PB, :], ALU.mult, ALU.mult, [xt, "nss", "gfin"], [("yo", blk % 2)])
            S.dma("pool", yout[blk * PB:(blk + 1) * PB, :], y[:PB, :], R=[("yo", blk % 2)], W=[])
        S.barrier()
        A.reset(m0)

    def ffn_phase(self, l, f, NT, BPT, PB):
        S, A = self.S, self.A
        S.barrier()
        m0 = A.mark()
        xnTs = [A.alloc(8 * 512, BF16).rearrange("p (k t) -> p k t", k=8) for _ in range(2)]
        hT = A.alloc(NM * 512, BF16).rearrange("p (m t) -> p m t", m=NM)
        self.sgb = [A.alloc(512) for _ in range(2)]
        sq, ss = A.alloc(D), A.alloc(8)
        xrs = [A.alloc(D, BF16) for _ in range(BPT)]
        gi = (l * 3 + (0 if f == 1 else 2)) * 8
        self.norm_stats(0, BPT, PB, sq, ss, xrs)
        self.norm_tr(xnTs[0], ("xnT", 0), gi, BPT, PB, xrs)
        for tt in range(NT):
            cur = tt % 2
            self.ffn_A(l, f, BPT, PB, xnTs[cur], ("xnT", cur), hT)
            if tt + 1 < NT:
                self.norm_stats((tt + 1) * BPT, BPT, PB, sq, ss, xrs)
            self.ffn_B(l, f, tt * BPT, BPT, PB, hT)
            if tt + 1 < NT:
                self.norm_tr(xnTs[1 - cur], ("xnT", 1 - cur), gi, BPT, PB, xrs)
        S.barrier()
        A.reset(m0)


    def mixer_phase(self, l, b, T, sample, NT, BPT, PB, C):
        mixer_phase_impl(self, l, b, T, sample, NT, BPT, PB, C)


def mixer_phase_impl(self, l, b, T, sample, NT, BPT, PB, C):
    S, A, I, O = self.S, self.A, self.I, self.O
    S.barrier()
    m0 = A.mark()
    NBLK = T // PB
    BM = 1 if T > 1024 else min(2, NBLK)
    TM = BM * PB
    NTM = NBLK // BM
    NCH = TM // C
    cs = 1 if sample else 0
    triU, strictL, onesblk, ones = (self.cm[:, cs, k, :] for k in (0, 1, 3, 4))
    identf = self.cm[:, cs, 5, :]
    identb = self.identb
    SC = 1.0 / float(np.sqrt(96.0))
    PAST = self.shp["PAST"] if sample else 0
    sfx = "s" if sample else "p"
    o_ckv, o_kpe, o_gdn, o_conv = (O[sfx + n][l, b] for n in ("ckv", "kpe", "gdn", "conv"))
    xnT = A.alloc(8 * TM, BF16).rearrange("p (k t) -> p k t", k=8)
    mixT = A.alloc(8 * TM, BF16).rearrange("p (k t) -> p k t", k=8)
    scr = (A.alloc(D), A.alloc(8))
    xrs = [A.alloc(D, BF16) for _ in range(BM)]
    qkvT = A.alloc(12 * TM).rearrange("p (c t) -> p c t", c=12)
    rw = [A.alloc(TM + 3) for _ in range(2)]
    cacc = A.alloc(TM); csab = [A.alloc(TM) for _ in range(2)]
    csq_all = scr[0] if 8 * TM <= D else A.alloc(8 * TM)
    rk = A.alloc(64)
    halo = A.alloc(36).rearrange("p (c j) -> p c j", j=3)
    wT = A.alloc(4 * SLOT, BF16).rearrange("p (j s) -> p j s", s=SLOT)
    wuq = A.alloc(2 * 768, BF16)
    wukv = A.alloc(1024, BF16)
    rot = A.alloc(NBLK * 32).rearrange("p (b c) -> p b c", c=32)
    KTt = A.alloc(8 * 512, BF16).rearrange("p (h t) -> p h t", h=8)
    Vt = A.alloc(4 * 8 * 65, BF16).rearrange("p (k h d) -> p k h d", k=4, h=8)
    QT = A.alloc(8 * TM, BF16).rearrange("p (h t) -> p h t", h=8)
    if sample:
        cst = A.alloc(4 * 128).rearrange("p (k d) -> p k d", k=4)
        kst = A.alloc(4 * 32).rearrange("p (k d) -> p k d", k=4)
        cbs = A.alloc(4 * 128, BF16).rearrange("p (k d) -> p k d", k=4)
        kbs = A.alloc(4 * 96, BF16).rearrange("p (k d) -> p k d", k=4)
        cT = A.alloc(512, BF16); kpT = A.alloc(512, BF16)
        cTn = A.alloc(T, BF16); kpTn = A.alloc(T, BF16)
    else:
        cT = A.alloc(T, BF16); kpT = A.alloc(T, BF16)
        cTn, kpTn = cT, kpT
    cqf = A.alloc(QR); cqn = A.alloc(QR, BF16); cqnT = A.alloc(2 * 128, BF16).rearrange("p (k t) -> p k t", k=2)
    qf = A.alloc(768); qb = A.alloc(768, BF16)
    rt = [A.alloc(8 * 16).rearrange("p (h d) -> p h d", h=8) for _ in range(4)]
    cf = A.alloc(KVR); cnew = [A.alloc(KVR) for _ in range(2)]; cb = A.alloc(KVR, BF16)
    kf = A.alloc(32); kn = [A.alloc(32) for _ in range(2)]; kb96 = A.alloc(96, BF16)
    sm = A.alloc(16)
    eT = [A.alloc(8 * 128, BF16).rearrange("p (h q) -> p h q", h=8) for _ in range(2)]
    oacc = [A.alloc(8 * 65).rearrange("p (h d) -> p h d", h=8) for _ in range(BM)]
    mo = A.alloc(512, BF16)
    mog = A.alloc(512, BF16)
    gball = A.alloc(14 * 512)
    gb = [gball[:, i * 512:(i + 1) * 512] for i in range(14)]
    Sst = A.alloc(512).rearrange("p (h d) -> p h d", h=8)
    gsm = A.alloc(64)
    qodd = A.alloc(8 * 64)
    cbuf = gball[:, 11 * 512:14 * 512]
    G = lambda i: gb[i]
    GT = lambda i: ("gb", i)
    g3 = lambda i, n: gb[i][:, 0:8 * n].rearrange("p (h d) -> p h d", h=8)

    for j in range(4):
        S.dma("sp", wT[:, j, :], self.wscr[self.slot_id[("T", l, j)]], R=[("wscr", self.slot_id[("T", l, j)])], W=["wT"])
    S.dma("sp", wuq, self.wscr[self.slot_id[("UQ", l)]][:, 0:1536], R=[("wscr", self.slot_id[("UQ", l)])], W=["wuq"])
    S.dma("sp", wukv, self.wscr[self.slot_id[("UKV", l)]][:, 0:1024], R=[("wscr", self.slot_id[("UKV", l)])], W=["wukv"])
    rsrc = I["rots"] if sample else I["rotp"]
    S.dma("pool", rot[:PB, :, :], rsrc.rearrange("(b p) c -> p b c", p=PB), W=["rot"])
    self.memset("pool", Vt[:, :, :, 64:65], 1.0, ["Vt1"])
    self.memset("pool", kb96[:, 0:64], 0.0, ["kb96"])
    if sample:
        self.memset("pool", kbs[:, :, 0:64], 0.0, ["kbs"])
        for j in range(3):
            S.dma("pool", halo[:, :, j], I["sc"][l, b, j].rearrange("(c p) -> p c", p=128), W=["halo"], slow=True)
        S.dma("pool", Sst[:64, :, :], I["sg"][l, b].rearrange("h k v -> k h v"), W=["S"])
    else:
        self.memset("pool", halo, 0.0, ["halo"])
        self.memset("pool", Sst[:64, :, :], 0.0, ["S"])
    gi = (l * 3 + 1) * 8
    dtb_bc, gnw_bc, qn_bc, kvn_bc = self.bc(l, "dtb"), self.bc(l, "gnw"), self.bc(l, "qn"), self.bc(l, "kvn")
    negA = self.negA[:, l * 8:(l + 1) * 8]

    def kvgen(cTt, kpTt, nk, tagk):
        for h in range(8):
            pb, pt = self.bank()
            self.mm(pb[:64, :nk], wukv[:, h * 128:h * 128 + 64], cTt[:, :nk], True, True, ["wukv", tagk], [pt])
            self.cp("act" if h % 2 else "dve", KTt[0:64, h, :nk], pb[:64, :nk], [pt], ["KTt"])
        self.cp("pool", KTt[64:96, :, :nk], kpTt[64:96, :nk].unsqueeze(1).to_broadcast([32, 8, nk]), [tagk], ["KTt"])
        wv = wukv.rearrange("p (h d) -> p h d", h=8)[:, :, 64:128]
        for kb in range((nk + 127) // 128):
            n = min(128, nk - kb * 128)
            pb, pt = self.bank()
            self.mm(pb[:n, :].rearrange("p (h d) -> p h d", h=8), cTt[:, kb * 128:kb * 128 + n], wv, True, True, ["wukv", tagk], [pt])
            self.cp("act" if kb % 2 else "dve", Vt[:n, kb, :, 0:64], pb[:n, :].rearrange("p (h d) -> p h d", h=8), [pt, "Vt1"], ["Vt"])

    def attend(nk, qblks, first, diag_kb):
        for qi, (q0, nq, kbmax) in enumerate(qblks):
            nkb = min((nk + 127) // 128, kbmax + 1)
            if nkb <= 0:
                continue
            po = [self.bank(hold=True), self.bank(hold=True)]

            def scores(kb):
                n = min(128, nk - kb * 128)
                e = eT[kb % 2]
                et = ("eT", kb % 2)
                for hb in range(2):
                    pb, pt = self.bank()
                    for hh in range(4):
                        h = hb * 4 + hh
                        self.mm(pb[:n, hh * 128:hh * 128 + nq], KTt[0:96, h, kb * 128:kb * 128 + n], QT[0:96, h, q0:q0 + nq], True, True, ["KTt", "QT"], [pt])
                    self.act(e[:n, hb * 4:hb * 4 + 4, :nq], pb[:n, :].rearrange("p (h q) -> p h q", h=4)[:, :, :nq], AF.Exp, [pt], [et], scale=SC)
                if diag_kb[qi] == kb:
                    self.memset("pool", e[64:128, :, 0:64], 0.0, [et])

            scores(0)
            for kb in range(nkb):
                if kb + 1 < nkb:
                    scores(kb + 1)
                n = min(128, nk - kb * 128)
                e = eT[kb % 2]
                et = ("eT", kb % 2)
                for h in range(8):
                    self.mm(po[h // 4][0][:nq, (h % 4) * 65:(h % 4) * 65 + 65], e[:n, h, :nq], Vt[:n, kb, h, :],
                            kb == 0 and h % 4 == 0, kb == nkb - 1 and h % 4 == 3, [et, "Vt"], [po[h // 4][1]])
            for hb in range(2):
                dst = oacc[qi][:nq, hb * 4:hb * 4 + 4, :]
                src = po[hb][0][:nq, 0:260].rearrange("p (h d) -> p h d", h=4)
                if first:
                    self.cp("dve", dst, src, [po[hb][1]], [("oacc", qi)])
                else:
                    self.tt("dve", dst, dst, src, ALU.add, [po[hb][1]], [("oacc", qi)])
            self.release(po[0][1], po[1][1])


    def mla_a(bi):
        blk = self.cur_blk0 + bi
        t0 = bi * PB
        pm, pmt = self.bank()
        for k in range(8):
            self.mm(pm[:PB, 0:416], xnT[:, k, t0:t0 + PB], wT[:, k // 2, (k % 2) * 944 + 528:(k % 2) * 944 + 944], k == 0, k == 7, ["wT", "xnT"], [pmt])
        self.cp("act", cqf[:PB, :], pm[:PB, 0:256], [pmt], ["cqf"])
        self.cp("act", cf[:PB, :], pm[:PB, 256:384], [pmt], ["cf"])
        self.cp("act", kf[:PB, :], pm[:PB, 384:416], [pmt], ["kf"])
        self.tt("dve", qf[:PB, 0:256], cqf[:PB, :], cqf[:PB, :], ALU.mult, ["cqf"], ["qf"])
        self.red(sm[:PB, 0:1], qf[:PB, 0:256], ["qf"], ["sm0"])
        self.rstd(sm[:PB, 0:1], 1.0 / QR, "sm0")
        self.stt("dve", cqn[:PB, :], cqf[:PB, :], sm[:PB, 0:1], qn_bc[:PB, :], ALU.mult, ALU.mult, ["cqf", "sm0", "const"], ["cqn"])
        cn = cnew[blk % 2]
        cnt = ("cnew", blk % 2)
        self.tt("dve", cn[:PB, :], cf[:PB, :], cf[:PB, :], ALU.mult, ["cf"], [cnt])
        self.red(sm[:PB, 1:2], cn[:PB, :], [cnt], ["sm1"])
        self.rstd(sm[:PB, 1:2], 1.0 / KVR, "sm1")
        self.stt("dve", cn[:PB, :], cf[:PB, :], sm[:PB, 1:2], kvn_bc[:PB, :], ALU.mult, ALU.mult, ["cf", "sm1", "const"], [cnt])
        S.dma("pool", o_ckv[blk * PB:(blk + 1) * PB, :], cn[:PB, :], R=[cnt], W=[])
        self.cp("act", cb[:PB, :], cn[:PB, :], [cnt], ["cb"])
        knn = kn[blk % 2]
        knt = ("kn", blk % 2)
        c1, s1 = rot[:PB, blk, 0:16], rot[:PB, blk, 16:32]
        k0, k1, k2, k3 = (rk[:PB, i_ * 16:(i_ + 1) * 16] for i_ in range(4))
        self.tt("dve", k0, kf[:PB, 0:16], c1, ALU.mult, ["kf", "rot"], ["rk0"])
        self.tt("pool", k1, kf[:PB, 16:32], s1, ALU.mult, ["kf", "rot"], ["rk1"])
        self.tt("dve", k2, kf[:PB, 16:32], c1, ALU.mult, ["kf", "rot"], ["rk2"])
        self.tt("pool", k3, kf[:PB, 0:16], s1, ALU.mult, ["kf", "rot"], ["rk3"])
        self.tt("dve", knn[:PB, 0:16], k0, k1, ALU.subtract, ["rk0", "rk1"], [knt])
        self.tt("dve", knn[:PB, 16:32], k2, k3, ALU.add, ["rk2", "rk3"], [knt])
        S.dma("pool", o_kpe[blk * PB:(blk + 1) * PB, :], knn[:PB, :], R=[knt], W=[])
        self.cp("act", kb96[:PB, 64:96], knn[:PB, :], [knt], ["kb96"])

    def mla_b1(bi):
        blk = self.cur_blk0 + bi
        t0 = bi * PB
        kc0 = t0 if sample else blk * PB
        cosb = rot[:PB, blk, 0:16].unsqueeze(1).to_broadcast([PB, 8, 16])
        sinb = rot[:PB, blk, 16:32].unsqueeze(1).to_broadcast([PB, 8, 16])
        pb, pt = self.bank()
        pbv = pb.bitcast(BF16)
        for k in range(2):
            self.tr(pbv[:, k * PB:(k + 1) * PB], cqn[:PB, k * 128:(k + 1) * 128], identb[:PB, :PB], ["cqn", "const"], [pt])
        self.cp("dve", cqnT[:, :, :PB], pbv[:, 0:2 * PB].rearrange("p (k t) -> p k t", k=2), [pt], ["cqnT"])
        pb2, pt2 = self.bank()
        pbv2 = pb2.bitcast(BF16)
        self.tr(pbv2[:, 0:PB], cb[:PB, :], identb[:PB, :PB], ["cb", "const"], [pt2])
        self.tr(pbv2[0:96, 128:128 + PB], kb96[:PB, :], identb[:PB, :PB], ["kb96", "const"], [pt2])
        self.cp("act", cTn[:, kc0:kc0 + PB], pbv2[:, 0:PB], [pt2], ["cTn"])
        self.cp("act", kpTn[64:96, kc0:kc0 + PB], pbv2[64:96, 128:128 + PB], [pt2], ["cTn"])
        q1, q1t = self.bank()
        q2, q2t = self.bank()
        for k in range(2):
            self.mm(q1[:PB, :], cqnT[:, k, :PB], wuq[:, k * 768:k * 768 + 512], k == 0, k == 1, ["cqnT", "wuq"], [q1t])
        for k in range(2):
            self.mm(q2[:PB, 0:256], cqnT[:, k, :PB], wuq[:, k * 768 + 512:k * 768 + 768], k == 0, k == 1, ["cqnT", "wuq"], [q2t])
        self.cp("act", qf[:PB, 0:512], q1[:PB, :], [q1t], ["qf"])
        self.cp("act", qf[:PB, 512:768], q2[:PB, 0:256], [q2t], ["qf"])
        qf3 = qf[:PB, :].rearrange("p (h d) -> p h d", h=8)
        qb3 = qb[:PB, :].rearrange("p (h d) -> p h d", h=8)
        x1, x2 = qf3[:, :, 64:80], qf3[:, :, 80:96]
        r0, r1, r2, r3 = (r_[:PB] for r_ in rt)
        self.tt("dve", r0, x1, cosb, ALU.mult, ["qf", "rot"], ["rt0"])
        self.tt("pool", r1, x2, sinb, ALU.mult, ["qf", "rot"], ["rt1"])
        self.tt("dve", r2, x2, cosb, ALU.mult, ["qf", "rot"], ["rt2"])
        self.tt("pool", r3, x1, sinb, ALU.mult, ["qf", "rot"], ["rt3"])
        self.tt("dve", qb3[:, :, 64:80], r0, r1, ALU.subtract, ["rt0", "rt1"], ["qb"])
        self.tt("dve", qb3[:, :, 80:96], r2, r3, ALU.add, ["rt2", "rt3"], ["qb"])
        self.cp("act", qb3[:, :, 0:64], qf3[:, :, 0:64], ["qf"], ["qb"])

    def mla_b2(bi):
        t0 = bi * PB
        qb3 = qb[:PB, :].rearrange("p (h d) -> p h d", h=8)
        pb, pt = self.bank()
        pbv = pb.bitcast(BF16)
        for h in range(8):
            self.tr(pbv[0:96, h * PB:(h + 1) * PB], qb3[:, h, :], identb[:PB, :PB], ["qb", "const"], [pt])
        self.cp("dve", QT[0:96, :, t0:t0 + PB], pbv[0:96, 0:8 * PB].rearrange("p (h t) -> p h t", h=8), [pt], ["QT"])

    for tm in range(NTM):
        blk0 = tm * BM
        self.cur_blk0 = blk0
        last = tm == NTM - 1
        self.norm_stats(blk0, BM, PB, scr[0], scr[1], xrs)
        self.norm_tr(xnT, "xnT", gi, BM, PB, xrs)
        if BM == 1 and STOP >= 3:
            mla_a(0)
        cbk = [self.bank(hold=True) for _ in range(3)] if last else None
        for j in range(6):
            w, wt = self.fetch(("F", l, j))
            for r in range(2):
                c = 2 * j + r
                pb, pt = self.bank()
                for k in range(8):
                    self.mm(pb[:, :TM], w[:, r * 1024 + k * 128:r * 1024 + (k + 1) * 128], xnT[:, k, :TM], k == 0, k == 7, [wt, "xnT"], [pt])
                if last:
                    for k in range(8):
                        self.mm(cbk[c // 4][0][0:3, (c % 4) * 128:(c % 4) * 128 + 128], xnT[:, k, TM - 3:TM],
                                w[:, r * 1024 + k * 128:r * 1024 + (k + 1) * 128], k == 0, k == 7, [wt, "xnT"], [cbk[c // 4][1]])
                rwc = rw[c % 2]
                rwt = ("rw", c % 2)
                self.cp("act", rwc[:, 0:3], halo[:, c, :], ["halo"], [rwt])
                self.cp("act", rwc[:, 3:3 + TM], pb[:, :TM], [pt], [rwt])
                self.cp("act", halo[:, c, :], rwc[:, TM:TM + 3], [rwt], ["halo"])
                cw = lambda jj, c=c: self.cwT[:, l * 48 + c * 4 + jj:l * 48 + c * 4 + jj + 1]
                self.ts("dve", cacc[:, :TM], rwc[:, 3:3 + TM], cw(3), None, ALU.mult, None, [rwt, "const"], ["cacc"])
                for jj in (2, 1, 0):
                    self.stt("dve", cacc[:, :TM], rwc[:, jj:jj + TM], cw(jj), cacc[:, :TM], ALU.mult, ALU.add, [rwt, "const", "cacc"], ["cacc"])
                cs_, cst_ = csab[c % 2], ("csa", c % 2)
                self.act(cs_[:, :TM], cacc[:, :TM], AF.Exp, ["cacc"], [cst_], scale=-1.0)
                self.ts("dve", cs_[:, :TM], cs_[:, :TM], 1.0, None, ALU.add, None, [cst_], [cst_])
                self.S.op("dve", lambda v, a=cs_[:, :TM]: v.reciprocal(out=a, in_=a), [cst_], [cst_])
                self.tt("dve", qkvT[:, c, :TM], cacc[:, :TM], cs_[:, :TM], ALU.mult, ["cacc", cst_], [("qkvT", c)])
                if c < 8:
                    self.tt("pool", csq_all[:, c * TM:(c + 1) * TM], qkvT[:, c, :TM], qkvT[:, c, :TM], ALU.mult, [("qkvT", c)], [("csq", c)])
        for c in range(8):
            p2, p2t = self.bank()
            crn = csq_all[:, c * TM:(c + 1) * TM]
            self.mm(p2[:, :TM], onesblk, crn, True, True, [("csq", c), "const"], [p2t])
            self.act(crn, p2[:, :TM], AF.Ln, [p2t], [("csq", c)], bias=EPS)
            self.act(crn, crn, AF.Exp, [("csq", c)], [("csq", c)], scale=-0.5)
            self.stt("dve", qkvT[:, c, :TM], qkvT[:, c, :TM], (0.125 if c < 4 else 1.0), crn, ALU.mult, ALU.mult, [("qkvT", c), ("csq", c)], [("qkvT", c)])

        if last:
            for k3 in range(3):
                self.cp("act", cbuf[0:3, k3 * 512:(k3 + 1) * 512], cbk[k3][0][0:3, :], [cbk[k3][1]], [("gb", 11), ("gb", 12), ("gb", 13)])
            S.dma("pool", o_conv, cbuf[0:3, :], R=[("gb", 11), ("gb", 12), ("gb", 13)], W=[])
            self.release(*[c_[1] for c_ in cbk])
        qkvR = [("qkvT", c) for c in range(12)]
        if STOP >= 3:
            if BM == 1:
                mla_b1(0)
            else:
                for bi in range(BM):
                    mla_a(bi)
                    mla_b1(bi)
                    mla_b2(bi)

        nlev = int(np.log2(C)) - 1
        g_, be_, eg_, er_, egl_, ssq_ = (gsm[:, i * 8:(i + 1) * 8] for i in range(6))

        def pe_pro(ch):
            t0 = ch * C
            pz, pzt = self.bank(hold=True)
            pab, pabt = self.bank(hold=True)
            for k in range(8):
                self.mm(pz[:C, :], xnT[:, k, t0:t0 + C], wT[:, k // 2, (k % 2) * 944:(k % 2) * 944 + 512], k == 0, k == 7, ["wT", "xnT"], [pzt])
            for k in range(8):
                self.mm(pab[:C, 0:16], xnT[:, k, t0:t0 + C], wT[:, k // 2, (k % 2) * 944 + 512:(k % 2) * 944 + 528], k == 0, k == 7, ["wT", "xnT"], [pabt])
            tb = []
            for src0 in (4, 8):
                pb, pt = self.bank(hold=True)
                for cc in range(4):
                    self.tr(pb[:C, cc * 128:(cc + 1) * 128], qkvT[:, src0 + cc, t0:t0 + C], identf, qkvR + ["const"], [pt])
                tb.append((pb, pt))
            return (pz, pzt), (pab, pabt), tb[0], tb[1]

        pro = pe_pro(0) if (NCH > 0 and STOP >= 4) else None
        if BM == 1 and STOP >= 3:
            mla_b2(0)
        og_pend = [None]
        for ch in range(NCH if STOP >= 4 else 0):
            t0 = ch * C
            (pz, pzt), (pab, pabt), (pkb, pkbt), (pvb, pvbt) = pro
            ZS, ATT, OO = 13, 11, 12
            self.act(G(ZS)[:C, :], pz[:C, :], AF.Exp, [pzt], [GT(ZS)], scale=-1.0)
            self.cp("act", gsm[:C, 48:64], pab[:C, 0:16], [pabt], ["abs"])
            self.release(pabt)
            self.ts("dve", G(ZS)[:C, :], G(ZS)[:C, :], 1.0, None, ALU.add, None, [GT(ZS)], [GT(ZS)])
            self.S.op("dve", lambda v, a=G(ZS)[:C, :]: v.reciprocal(out=a, in_=a), [GT(ZS)], [GT(ZS)])
            self.tt("dve", G(ZS)[:C, :], G(ZS)[:C, :], pz[:C, :], ALU.mult, [GT(ZS), pzt], [GT(ZS)])
            self.release(pzt)
            self.tt("dve", g_[:C, :], gsm[:C, 48:56], dtb_bc[:C, :], ALU.add, ["abs", "const"], ["g"])
            self.act(g_[:C, :], g_[:C, :], AF.Exp, ["g"], ["g"])
            self.act(g_[:C, :], g_[:C, :], AF.Ln, ["g"], ["g"], bias=1.0)
            self.tt("dve", g_[:C, :], g_[:C, :], negA[:C, :], ALU.mult, ["g", "negA"], ["g"])
            self.act(be_[:C, :], gsm[:C, 56:64], AF.Exp, ["abs"], ["beta"], scale=-1.0)
            self.ts("dve", be_[:C, :], be_[:C, :], 1.0, None, ALU.add, None, ["beta"], ["beta"])
            self.S.op("dve", lambda v, a=be_[:C, :]: v.reciprocal(out=a, in_=a), ["beta"], ["beta"])
            self.cp("act", G(9)[:C, :], pkb[:C, :], [pkbt], [GT(9)])
            self.release(pkbt)
            self.cp("act", G(10)[:C, :], pvb[:C, :], [pvbt], [GT(10)])
            self.release(pvbt)
            pq, pqt = self.bank()
            for cc in range(8):
                self.mm(pq[:64, cc * C:(cc + 1) * C], identf[:, 64:128], qkvT[:, cc, t0:t0 + C], True, True, qkvR + ["const"], [pqt])
            self.cp("act", qodd[:64, 0:8 * C], pq[:64, 0:8 * C], [pqt], ["qodd"])
            kTh = lambda h: qkvT[0:64, 4 + h // 2, t0:t0 + C] if h % 2 == 0 else qodd[:64, (4 + h // 2) * C:(5 + h // 2) * C]
            qTh = lambda h: qkvT[0:64, h // 2, t0:t0 + C] if h % 2 == 0 else qodd[:64, (h // 2) * C:(h // 2 + 1) * C]
            bbc = lambda n: be_[:C, :].unsqueeze(2).to_broadcast([C, 8, n])
            self.tt("dve", g3(0, C)[:C], strictL[:C, :C].unsqueeze(1).to_broadcast([C, 8, C]), g_[:C, :].unsqueeze(2).to_broadcast([C, 8, C]), ALU.mult, ["g", "const"], [GT(0)])
            pK, pKt = self.bank(hold=True)
            for h in range(8):
                self.mm(pK[:C, h * C:(h + 1) * C], kTh(h), kTh(h), True, True, qkvR + ["qodd"], [pKt])
            pa, pat = self.bank(hold=True)
            for h in range(8):
                self.mm(pa[:C, h * C:(h + 1) * C], kTh(h), qTh(h), True, True, qkvR + ["qodd"], [pat])
            pd, pdt = self.bank()
            self.mm(pd[:C, 0:8 * C], triU[:C, :C], G(0)[:C, 0:8 * C], True, True, [GT(0), "const"], [pdt])
            pdT, pdTt = self.bank()
            for h in range(8):
                self.mm(pdT[:C, h * C:(h + 1) * C], g3(0, C)[:C, h, :], triU[:C, :C], True, True, [GT(0), "const"], [pdTt])
            pg, pgt = self.bank()
            gb16 = gsm[:C, 0:16]
            self.mm(pg[:C, 0:16], triU[:C, :C], gb16, True, True, ["g", "beta", "const"], [pgt])
            self.mm(pg[:C, 16:32], strictL[:C, :C], gb16, True, True, ["g", "beta", "const"], [pgt])
            self.mm(pg[:64, 32:48], ones[:C, :64], gb16, True, True, ["g", "beta", "const"], [pgt])
            self.act(G(1)[:C, 0:8 * C], pd[:C, 0:8 * C], AF.Exp, [pdt], [GT(1)])
            self.act(G(2)[:C, 0:8 * C], pdT[:C, 0:8 * C], AF.Exp, [pdTt], [GT(2)])
            self.act(eg_[:C, :], pg[:C, 0:8], AF.Exp, [pgt], ["eg"])
            self.act(er_[:C, :], pg[:C, 16:24], AF.Exp, [pgt], ["eg"])
            self.act(egl_[:64, :], pg[:64, 32:40], AF.Exp, [pgt], ["egl"])
            self.tt("dve", g3(1, C)[:C], g3(1, C)[:C], strictL[:C, :C].unsqueeze(1).to_broadcast([C, 8, C]), ALU.mult, [GT(1), "const"], [GT(1)])
            self.tt("dve", g3(1, C)[:C], g3(1, C)[:C], bbc(C), ALU.mult, [GT(1), "beta"], [GT(1)])
            self.tt("dve", G(3)[:C, 0:8 * C], pK[:C, 0:8 * C], G(1)[:C, 0:8 * C], ALU.mult, [pKt, GT(1)], [GT(3)])
            self.release(pKt)
            if og_pend[0] is not None:
                og_pend[0]()
                og_pend[0] = None
            pL, pLt = self.bank()
            for h in range(8):
                self.tr(pL[:C, h * C:(h + 1) * C], g3(3, C)[:C, h, :], identf[:C, :C], [GT(3), "const"], [pLt])
            self.cp("act", G(4)[:C, 0:8 * C], pL[:C, 0:8 * C], [pLt], [GT(4)])
            self.tt("dve", g3(7, C)[:C], identf[:C, :C].unsqueeze(1).to_broadcast([C, 8, C]), g3(4, C)[:C], ALU.subtract, [GT(4), "const"], [GT(7)])
            self.tt("pool", g3(2, C)[:C], g3(2, C)[:C], triU[:C, :C].unsqueeze(1).to_broadcast([C, 8, C]), ALU.mult, [GT(2), "const"], [GT(2)])
            self.tt("dve", G(ATT)[:C, 0:8 * C], pa[:C, 0:8 * C], G(2)[:C, 0:8 * C], ALU.mult, [pat, GT(2)], [GT(ATT)])
            self.release(pat)
            egb = eg_[:C, :].unsqueeze(2).to_broadcast([C, 8, 64])
            erb = er_[:C, :].unsqueeze(2).to_broadcast([C, 8, 64])
            KBE, VB, KDEC = 0, 10, 9
            self.tt("pool", g3(KBE, 64)[:C], g3(9, 64)[:C], bbc(64), ALU.mult, [GT(9), "beta"], [GT(KBE)])
            self.tt("pool", g3(KBE, 64)[:C], g3(KBE, 64)[:C], egb, ALU.mult, [GT(KBE), "eg"], [GT(KBE)])
            self.tt("pool", g3(VB, 64)[:C], g3(VB, 64)[:C], bbc(64), ALU.mult, [GT(VB), "beta"], [GT(VB)])
            self.tt("pool", g3(KDEC, 64)[:C], g3(KDEC, 64)[:C], erb, ALU.mult, [GT(KDEC), "eg"], [GT(KDEC)])
            Pi, PTi, Ii = 3, 4, 7
            for lev in range(1, nlev + 1):
                Pn, PTn, In = (5 if Pi == 3 else 3), (6 if PTi == 4 else 4), (8 if Ii == 7 else 7)
                pP, pPt = self.bank()
                for h in range(8):
                    self.mm(pP[:C, h * C:(h + 1) * C], g3(PTi, C)[:C, h, :], g3(Pi, C)[:C, h, :], True, True, [GT(Pi), GT(PTi)], [pPt])
                if lev < nlev:
                    pQ, pQt = self.bank()
                    for h in range(8):
                        self.mm(pQ[:C, h * C:(h + 1) * C], g3(Pi, C)[:C, h, :], g3(PTi, C)[:C, h, :], True, True, [GT(Pi), GT(PTi)], [pQt])
                self.cp("act", G(Pn)[:C, 0:8 * C], pP[:C, 0:8 * C], [pPt], [GT(Pn)])
                if lev < nlev:
                    self.cp("dve", G(PTn)[:C, 0:8 * C], pQ[:C, 0:8 * C], [pQt], [GT(PTn)])
                pI, pIt = self.bank()
                for h in range(8):
                    self.mm(pI[:C, h * C:(h + 1) * C], g3(Pn, C)[:C, h, :], g3(Ii, C)[:C, h, :], True, True, [GT(Pn), GT(Ii)], [pIt])
                self.tt("dve", G(In)[:C, 0:8 * C], G(Ii)[:C, 0:8 * C], pI[:C, 0:8 * C], ALU.add, [GT(Ii), pIt], [GT(In)])
                Pi, PTi, Ii = Pn, PTn, In
            IT = Ii
            free = [i for i in (3, 4, 5, 6, 7, 8) if i != IT]
            VAL, KCT, VNEW, SQ = free[0], free[1], free[2], free[3]
            pv, pvt = self.bank()
            for h in range(8):
                self.mm(pv[:C, h * 64:(h + 1) * 64], g3(IT, C)[:C, h, :], g3(VB, 64)[:C, h, :], True, True, [GT(IT), GT(VB)], [pvt])
            pkc, pkct = self.bank()
            for h in range(8):
                self.mm(pkc[:64, h * C:(h + 1) * C], g3(KBE, 64)[:C, h, :], g3(IT, C)[:C, h, :], True, True, [GT(IT), GT(KBE)], [pkct])
            po1, po1t = self.bank()
            for h in range(8):
                self.mm(po1[:C, h * 64:(h + 1) * 64], qTh(h), Sst[:64, h, :], True, True, qkvR + ["S", "qodd"], [po1t])
            self.cp("act", G(KCT)[:64, 0:8 * C], pkc[:64, 0:8 * C], [pkct], [GT(KCT)])
            self.cp("act", G(VAL)[:C, :], pv[:C, :], [pvt], [GT(VAL)])
            pp, ppt = self.bank()
            for h in range(8):
                self.mm(pp[:C, h * 64:(h + 1) * 64], g3(KCT, C)[:64, h, :], Sst[:64, h, :], True, True, [GT(KCT), "S"], [ppt])
            self.tt("dve", G(VNEW)[:C, :], G(VAL)[:C, :], pp[:C, :], ALU.subtract, [GT(VAL), ppt], [GT(VNEW)])
            po2, po2t = self.bank()
            for h in range(8):
                self.mm(po2[:C, h * 64:(h + 1) * 64], g3(ATT, C)[:C, h, :], g3(VNEW, 64)[:C, h, :], True, True, [GT(ATT), GT(VNEW)], [po2t])
            pS, pSt = self.bank()
            for h in range(8):
                self.mm(pS[:64, h * 64:(h + 1) * 64], g3(KDEC, 64)[:C, h, :], g3(VNEW, 64)[:C, h, :], True, True, [GT(KDEC), GT(VNEW)], [pSt])
            self.tt("dve", g3(OO, 64)[:C], po1[:C, :].rearrange("p (h d) -> p h d", h=8), egb, ALU.mult, [po1t, "eg"], [GT(OO)])
            self.tt("dve", G(OO)[:C, :], G(OO)[:C, :], po2[:C, :], ALU.add, [GT(OO), po2t], [GT(OO)])
            self.tt("dve", G(SQ)[:C, :], G(OO)[:C, :], G(OO)[:C, :], ALU.mult, [GT(OO)], [GT(SQ)])
            self.red(ssq_[:C, :], g3(SQ, 64)[:C], [GT(SQ)], ["ssq"])
            self.rstd(ssq_[:C, :], 1.0 / 64, "ssq")
            self.tt("dve", G(ZS)[:C, :].rearrange("p (h d) -> p h d", h=8), G(ZS)[:C, :].rearrange("p (h d) -> p h d", h=8),
                    gnw_bc[:C, :].unsqueeze(1).to_broadcast([C, 8, 64]), ALU.mult, [GT(ZS), "const"], [GT(ZS)])
            self.tt("dve", g3(OO, 64)[:C], g3(OO, 64)[:C], ssq_[:C, :].unsqueeze(2).to_broadcast([C, 8, 64]), ALU.mult, [GT(OO), "ssq"], [GT(OO)])
            self.tt("dve", mog[:C, :], G(OO)[:C, :], G(ZS)[:C, :], ALU.mult, [GT(OO), GT(ZS)], ["mog"])
            self.tt("pool", Sst[:64], Sst[:64], egl_[:64, :].unsqueeze(2).to_broadcast([64, 8, 64]), ALU.mult, ["S", "egl"], ["S"])
            self.tt("dve", Sst[:64], Sst[:64], pS[:64, :].rearrange("p (h d) -> p h d", h=8), ALU.add, ["S", pSt], ["S"])
            if ch + 1 < NCH:
                pro = pe_pro(ch + 1)
            def og_emit(t0=t0):
                pb, pt = self.bank()
                pbv = pb.bitcast(BF16)
                for cc in range(4):
                    self.tr(pbv[:, cc * C:(cc + 1) * C], mog[:C, cc * 128:(cc + 1) * 128], identb[:C, :C], ["mog", "const"], [pt])
                self.cp("act", mixT[:, 0:4, t0:t0 + C], pbv[:, 0:4 * C].rearrange("p (k t) -> p k t", k=4), [pt], ["mixT"])
            og_pend[0] = og_emit

        if STOP < 5:
            continue
        if not sample:
            qblks_all = []
            for kt in range((blk0 + BM - 1) // 4 + 1):
                nk = min(512, (blk0 + BM) * PB - kt * 512)
                kvgen(cT[:, kt * 512:kt * 512 + nk], kpT[:, kt * 512:kt * 512 + nk], nk, "cTn")
                qblks = []
                diag = []
                for bi in range(BM):
                    gq = blk0 + bi
                    kbmax = gq - kt * 4
                    qblks.append((bi * PB, PB, min(kbmax, 3)))
                    diag.append(kbmax if 0 <= kbmax <= 3 else None)
                attend(nk, qblks, kt == 0, diag)
        else:
            npt = PAST // 512
            for kt in range(npt):
                S.dma("pool", cst, I["cckv"][l, b, kt * 512:(kt + 1) * 512, :].rearrange("(k p) d -> p k d", p=128), W=["cst"])
                S.dma("pool", kst, I["ckr"][l, b, kt * 512:(kt + 1) * 512, :].rearrange("(k p) d -> p k d", p=128), W=["kst"])
                self.cp("act", cbs, cst, ["cst"], ["cbs"])
                self.cp("dve", kbs[:, :, 64:96], kst, ["kst", "kbs"], ["kbs"])
                pb, pt = self.bank()
                pbv = pb.bitcast(BF16)
                for k in range(4):
                    self.tr(pbv[:, k * 128:(k + 1) * 128], cbs[:, k, :], identb, ["cbs", "const"], [pt])
                self.cp("dve", cT[:, 0:512], pbv[:, 0:512], [pt], ["cT"])
                pb, pt = self.bank()
                pbv = pb.bitcast(BF16)
                for k in range(4):
                    self.tr(pbv[0:96, k * 128:(k + 1) * 128], kbs[:, k, :], identb, ["kbs", "const"], [pt])
                self.cp("dve", kpT[64:96, 0:512], pbv[64:96, 0:512], [pt], ["cT"])
                kvgen(cT[:, 0:512], kpT[:, 0:512], 512, "cT")
                attend(512, [(0, T, 3)], kt == 0, [None])
            kvgen(cTn[:, 0:T], kpTn[:, 0:T], T, "cTn")
            attend(T, [(0, T, 0)], npt == 0, [None])
        if og_pend[0] is not None:
            og_pend[0]()
            og_pend[0] = None
        for bi in range(BM):
            self.S.op("dve", lambda v, o=sm[:PB, 8:16], i=oacc[bi][:PB, :, 64]: v.reciprocal(out=o, in_=i), [("oacc", bi)], ["sm8"])
            self.tt("dve", mo[:PB, :].rearrange("p (h d) -> p h d", h=8), oacc[bi][:PB, :, 0:64], sm[:PB, 8:16].unsqueeze(2).to_broadcast([PB, 8, 64]),
                    ALU.mult, [("oacc", bi), "sm8"], ["mo"])
            pb, pt = self.bank()
            pbv = pb.bitcast(BF16)
            for cc in range(4):
                self.tr(pbv[:, cc * PB:(cc + 1) * PB], mo[:PB, cc * 128:(cc + 1) * 128], identb[:PB, :PB], ["mo", "const"], [pt])
            self.cp("act", mixT[:, 4:8, bi * PB:(bi + 1) * PB], pbv[:, 0:4 * PB].rearrange("p (k t) -> p k t", k=4), [pt], ["mixT"])
        for half in range(2):
            banks = [self.bank(hold=True) for _ in range(BM)]
            for j in range(4):
                w, wt = self.fetch(("O", l, j))
                for r in range(2):
                    kc = 2 * j + r
                    for bi in range(BM):
                        self.mm(banks[bi][0][:PB, :], mixT[:, kc, bi * PB:(bi + 1) * PB], w[:, r * 1024 + half * 512:r * 1024 + half * 512 + 512],
                                kc == 0, kc == 7, [wt, "mixT"], [banks[bi][1]])
            for bi in range(BM):
                xs = self.X[:PB, blk0 + bi, half * 512:(half + 1) * 512]
                self.tt("dve", xs, xs, banks[bi][0][:PB, :], ALU.add, [banks[bi][1]], [("X", blk0 + bi)])
                self.release(banks[bi][1])
    S.dma("pool", o_gdn.rearrange("h k v -> k h v"), Sst[:64, :, :], R=["S"], W=[])
    S.barrier()
    A.reset(m0)


def _consts(C_p, C_s):
    out = np.zeros((2, 6, 128, 128), np.float32)
    idx = np.arange(128)
    for s, Cc in enumerate((C_p, C_s)):
        same = (idx[:, None] // Cc) == (idx[None, :] // Cc)
        out[s, 0] = ((idx[:, None] <= idx[None, :]) & same)
        out[s, 1] = ((idx[:, None] > idx[None, :]) & same)
        out[s, 2] = ((idx[:, None] < idx[None, :]) & same)
        out[s, 3] = (idx[:, None] // 64) == (idx[None, :] // 64)
        out[s, 4] = 1.0
        out[s, 5] = np.eye(128)
    return out


def _rot_table(pos):
    half = ROPE // 2
    inv_freq = (1.0 / (np.float32(10000.0) ** (np.arange(half, dtype=np.float32) / np.float32(half)))).astype(np.float32)
    ang = pos.astype(np.float32)[:, None] * inv_freq[None, :]
    return np.concatenate([np.cos(ang), np.sin(ang)], axis=1).astype(np.float32)


_CACHE = {}


def _get_program(shp):
    key = tuple(sorted(shp.items()))
    if key in _CACHE:
        return _CACHE[key]
    nc0 = bass.Bass("TRN2", target_bir_lowering=False)
    b0 = Builder(nc0, shp, dry=True)
    b0.build()
    plan = b0.req
    nc = bass.Bass("TRN2", target_bir_lowering=False)
    bld = Builder(nc, shp, dry=False, plan=plan)
    bld.build()
    S = bld.S
    with nc.Block() as block:
        @block.tensor
        def _(e):
            S.replay("pe", e)

        @block.scalar
        def _(e):
            S.replay("act", e)

        @block.vector
        def _(e):
            S.replay("dve", e)

        @block.gpsimd
        def _(e):
            S.replay("pool", e)

        @block.sync
        def _(e):
            S.replay("sp", e)
    _CACHE[key] = nc
    return nc


def kernel(x_prompt, x_sample, cache_mla_ckv, cache_mla_krope, state_gdn, state_gdn_conv,
           norm_ffn1, w_ffn1_gate, w_ffn1_up, w_ffn1_down, norm_mix, w_in, gdn_conv_w, gdn_a_log,
           gdn_dt_bias, gdn_norm_w, mla_q_norm, mla_kv_norm, w_uq, w_ukv, w_out, norm_ffn2,
           w_ffn2_gate, w_ffn2_up, w_ffn2_down, norm_final):
    f = lambda a: np.ascontiguousarray(np.asarray(a, dtype=np.float32))
    x_prompt, x_sample = f(x_prompt), f(x_sample)
    B, T, _ = x_prompt.shape
    BS, TS, _ = x_sample.shape
    depth = w_in.shape[0]
    PAST = cache_mla_ckv.shape[2]
    NP, NS = B // NCORES, BS // NCORES
    shp = dict(depth=depth, NP=NP, T=T, NS=NS, TS=TS, PAST=PAST)
    nc = _get_program(shp)
    C_p = 64 if T % 64 == 0 else T
    C_s = 64 if TS % 64 == 0 else TS
    common = dict(
        nf1=f(norm_ffn1), wg1=f(w_ffn1_gate), wu1=f(w_ffn1_up), wd1=f(w_ffn1_down),
        nf2=f(norm_ffn2), wg2=f(w_ffn2_gate), wu2=f(w_ffn2_up), wd2=f(w_ffn2_down),
        nm=f(norm_mix), win=f(w_in), cw=f(gdn_conv_w), alog=f(gdn_a_log), dtb=f(gdn_dt_bias),
        gnw=f(gdn_norm_w), qn=f(mla_q_norm), kvn=f(mla_kv_norm), wuq=f(w_uq), wukv=f(w_ukv),
        wo=f(w_out), nfin=f(norm_final).reshape(1, D),
        rotp=_rot_table(np.arange(T)), rots=_rot_table(PAST + np.arange(TS)),
        cmask=_consts(C_p, C_s), identb=np.eye(128, dtype=np.float32).astype(ml_dtypes.bfloat16),
    )
    ckv, ckr, sg, sc = f(cache_mla_ckv), f(cache_mla_krope), f(state_gdn), f(state_gdn_conv)
    in_maps = []
    for c in range(NCORES):
        m = dict(common)
        m["xp"] = x_prompt[c * NP:(c + 1) * NP]
        m["xs"] = x_sample[c * NS:(c + 1) * NS]
        m["cckv"] = np.ascontiguousarray(ckv[:, c * NS:(c + 1) * NS])
        m["ckr"] = np.ascontiguousarray(ckr[:, c * NS:(c + 1) * NS])
        m["sg"] = np.ascontiguousarray(sg[:, c * NS:(c + 1) * NS])
        m["sc"] = np.ascontiguousarray(sc[:, c * NS:(c + 1) * NS])
        in_maps.append(m)
    res = run_bass_kernel_spmd(nc, in_maps, core_ids=list(range(NCORES)))
    R = res.results
    cat = lambda k, ax: np.concatenate([np.asarray(r[k], dtype=np.float32) for r in R], axis=ax)
    return (cat("yp", 0), cat("ys", 0), cat("pckv", 1), cat("pkpe", 1), cat("pgdn", 1), cat("pconv", 1),
            cat("sckv", 1), cat("skpe", 1), cat("sgdn", 1), cat("sconv", 1))
```
